# Optimizing a Trainium2 kernel written in Bass

```python
import math
import jax, jax.numpy as jnp
from jax import lax
import numpy as np

D_MODEL = 4096
BATCH = 2
SEQ = 4096
DEPTH = 2

EXPAND = 2
MIX_WIDTH = EXPAND * D_MODEL
S5_WIDTH = MIX_WIDTH // 4
S5_GROUP = 16
S5_GROUPS = S5_WIDTH // S5_GROUP
S5_STATE = 64
S5_EIG_CLIP = -1e-4
SSD_WIDTH = MIX_WIDTH - S5_WIDTH
SSD_HEAD_DIM = 64
SSD_HEADS = SSD_WIDTH // SSD_HEAD_DIM
SSD_GROUPS = 8
SSD_STATE = 128
SSD_CONV = 4
SSD_CHUNK = 128
SSD_XBC = SSD_WIDTH + 2 * SSD_GROUPS * SSD_STATE
FOX_HEAD_DIM = 128
FOX_HEADS = D_MODEL // FOX_HEAD_DIM
FOX_WIDTH = FOX_HEADS * FOX_HEAD_DIM
FOX_BLOCK = 128
NORM_EPS = 1e-5

EVEN_IN = 2 * S5_WIDTH + SSD_WIDTH + SSD_XBC + SSD_HEADS
ODD_IN = 4 * FOX_WIDTH + FOX_HEADS

kernel_name = "hybrid_s5_ssd_fox_trunk"

F32 = jnp.float32


def rms_norm(x, w):
    xf = x.astype(F32)
    y = xf * lax.rsqrt(jnp.mean(xf * xf, axis=-1, keepdims=True) + NORM_EPS)
    return (y * w.astype(F32)).astype(x.dtype)


def causal_depthwise_conv(x, w, b):
    k_width, ch = w.shape
    y = lax.conv_general_dilated(
        x, w.astype(F32)[:, None, :], window_strides=(1,),
        padding=((k_width - 1, 0),), dimension_numbers=("NWC", "WIO", "NWC"),
        feature_group_count=ch)
    return y + b.astype(F32)


def s5_mixer(u, lam_re, lam_im, log_step, b_re, b_im, c_re, c_im, d, w_glu, b_glu):
    bsz, seqlen, _ = u.shape
    u = u.reshape(bsz, seqlen, S5_GROUPS, S5_GROUP)
    lr = jnp.minimum(lam_re.astype(F32), S5_EIG_CLIP)
    li = lam_im.astype(F32)
    step = jnp.exp(log_step.astype(F32))[:, None]
    mag = jnp.exp(lr * step)
    ab_re = mag * jnp.cos(li * step)
    ab_im = mag * jnp.sin(li * step)
    denom = lr * lr + li * li
    nr = ab_re - 1.0
    ni = ab_im
    coef_re = (nr * lr + ni * li) / denom
    coef_im = (ni * lr - nr * li) / denom
    br = b_re.astype(F32)
    bi = b_im.astype(F32)
    bb_re = coef_re[..., None] * br - coef_im[..., None] * bi
    bb_im = coef_re[..., None] * bi + coef_im[..., None] * br
    bu_re = jnp.einsum('blgh,gph->blgp', u, bb_re)
    bu_im = jnp.einsum('blgh,gph->blgp', u, bb_im)
    a_re = jnp.broadcast_to(ab_re, bu_re.shape)
    a_im = jnp.broadcast_to(ab_im, bu_im.shape)

    def combine(e_i, e_j):
        ar_i, ai_i, br_i, bi_i = e_i
        ar_j, ai_j, br_j, bi_j = e_j
        return (ar_j * ar_i - ai_j * ai_i,
                ar_j * ai_i + ai_j * ar_i,
                ar_j * br_i - ai_j * bi_i + br_j,
                ar_j * bi_i + ai_j * br_i + bi_j)

    _, _, s_re, s_im = lax.associative_scan(combine, (a_re, a_im, bu_re, bu_im), axis=1)
    y = (jnp.einsum('blgp,ghp->blgh', s_re, c_re.astype(F32))
         - jnp.einsum('blgp,ghp->blgh', s_im, c_im.astype(F32))
         + d.astype(F32) * u)
    y = y.reshape(bsz, seqlen, S5_WIDTH)
    g = jax.nn.gelu(y)
    return g * jax.nn.sigmoid(g @ w_glu.astype(F32) + b_glu.astype(F32))


def ssd_chunked(x, dt, a_head, bm, cm):
    bsz, seqlen, nh, hd = x.shape
    ng, ns = bm.shape[2], bm.shape[3]
    r = nh // ng
    q = SSD_CHUNK
    nc = seqlen // q
    xc = (x * dt[..., None]).reshape(bsz, nc, q, ng, r, hd)
    la = (dt * a_head).reshape(bsz, nc, q, ng, r).transpose(0, 3, 4, 1, 2)
    la_cum = jnp.cumsum(la, axis=-1)
    bc = bm.reshape(bsz, nc, q, ng, ns)
    cc = cm.reshape(bsz, nc, q, ng, ns)
    causal = jnp.tril(jnp.ones((q, q), dtype=bool))
    seg = la_cum[..., :, None] - la_cum[..., None, :]
    decay_in = jnp.exp(jnp.where(causal, seg, -jnp.inf))
    scores = jnp.einsum('bcqgn,bckgn->bgcqk', cc, bc)
    w_in = scores[:, :, None] * decay_in
    y_diag = jnp.einsum('bgrcqk,bckgrp->bcqgrp', w_in, xc)
    decay_end = jnp.exp(la_cum[..., -1:] - la_cum).transpose(0, 3, 4, 1, 2)
    states = jnp.einsum('bckgn,bckgrp->bcgrpn', bc, xc * decay_end[..., None])
    chunk_decay = la_cum[..., -1]
    cs = jnp.cumsum(jnp.pad(chunk_decay, ((0, 0), (0, 0), (0, 0), (1, 0))), axis=-1)
    seg_c = cs[..., :, None] - cs[..., None, :]
    mask_c = jnp.tril(jnp.ones((nc + 1, nc + 1), dtype=bool))
    decay_c = jnp.exp(jnp.where(mask_c, seg_c, -jnp.inf))
    states_cat = jnp.concatenate([jnp.zeros_like(states[:, :1]), states], axis=1)
    states_in = jnp.einsum('bgrzc,bcgrpn->bzgrpn', decay_c[..., :nc, :], states_cat)
    decay_out = jnp.exp(la_cum).transpose(0, 3, 4, 1, 2)
    y_off = jnp.einsum('bcqgn,bcgrpn->bcqgrp', cc, states_in) * decay_out[..., None]
    return (y_diag + y_off).reshape(bsz, seqlen, nh, hd)


def ssd_mixer(z, xbc, dt_raw, conv_w, conv_b, dt_bias, a_log, d, norm_w):
    bsz, seqlen, _ = z.shape
    xbc = jax.nn.silu(causal_depthwise_conv(xbc, conv_w, conv_b))
    xs, bm, cm = jnp.split(xbc, [SSD_WIDTH, SSD_WIDTH + SSD_GROUPS * SSD_STATE], axis=-1)
    xs = xs.reshape(bsz, seqlen, SSD_HEADS, SSD_HEAD_DIM)
    bm = bm.reshape(bsz, seqlen, SSD_GROUPS, SSD_STATE)
    cm = cm.reshape(bsz, seqlen, SSD_GROUPS, SSD_STATE)
    dt = jax.nn.softplus(dt_raw + dt_bias.astype(F32))
    a_head = -jnp.exp(a_log.astype(F32))
    y = ssd_chunked(xs, dt, a_head, bm, cm) + d.astype(F32)[:, None] * xs
    y = y.reshape(bsz, seqlen, SSD_WIDTH) * jax.nn.silu(z)
    yg = y.reshape(bsz, seqlen, SSD_GROUPS, SSD_WIDTH // SSD_GROUPS)
    yg = yg * lax.rsqrt(jnp.mean(yg * yg, axis=-1, keepdims=True) + NORM_EPS)
    return yg.reshape(bsz, seqlen, SSD_WIDTH) * norm_w.astype(F32)


def fox_attention(q, k, v, f_logit, b_f):
    bsz, seqlen, nh, hd = q.shape
    log_f = jax.nn.log_sigmoid(f_logit + b_f.astype(F32))
    c = jnp.cumsum(log_f, axis=1).transpose(0, 2, 1)
    nb = seqlen // FOX_BLOCK
    qb = q.reshape(bsz, nb, FOX_BLOCK, nh, hd).transpose(1, 0, 2, 3, 4)
    cb = c.reshape(bsz, nh, nb, FOX_BLOCK).transpose(2, 0, 1, 3)
    kpos = jnp.arange(seqlen)
    scale = 1.0 / math.sqrt(FOX_HEAD_DIM)

    def block(args):
        qi, ci, i = args
        s = jnp.einsum('bqhd,bkhd->bhqk', qi, k) * scale + (ci[..., :, None] - c[:, :, None, :])
        qpos = i * FOX_BLOCK + jnp.arange(FOX_BLOCK)
        s = jnp.where(kpos[None, :] <= qpos[:, None], s, -jnp.inf)
        p = jax.nn.softmax(s, axis=-1)
        return jnp.einsum('bhqk,bkhd->bqhd', p, v)

    out = lax.map(block, (qb, cb, jnp.arange(nb)))
    return out.transpose(1, 0, 2, 3, 4).reshape(bsz, seqlen, nh * hd)


def ssm_layer(x, norm_w, w_in, lam_re, lam_im, log_step, b_re, b_im, c_re, c_im, s5_d,
              w_glu, b_glu, conv_w, conv_b, dt_bias, a_log, ssd_d, ssd_norm_w, w_out):
    h = rms_norm(x, norm_w)
    proj = (h @ w_in).astype(F32)
    s5_u, s5_gate, ssd_z, ssd_xbc, ssd_dt = jnp.split(
        proj, [S5_WIDTH, 2 * S5_WIDTH, 2 * S5_WIDTH + SSD_WIDTH,
               2 * S5_WIDTH + SSD_WIDTH + SSD_XBC], axis=-1)
    s5_out = s5_mixer(s5_u, lam_re, lam_im, log_step, b_re, b_im, c_re, c_im, s5_d,
                      w_glu, b_glu) * jax.nn.silu(s5_gate)
    ssd_out = ssd_mixer(ssd_z, ssd_xbc, ssd_dt, conv_w, conv_b, dt_bias, a_log, ssd_d, ssd_norm_w)
    mixed = jnp.concatenate([s5_out, ssd_out], axis=-1).astype(x.dtype)
    return mixed @ w_out


def fox_layer(x, norm_w, w_in, b_f, w_out):
    bsz, seqlen, _ = x.shape
    h = rms_norm(x, norm_w)
    proj = (h @ w_in).astype(F32)
    q, k, v, gate, f_logit = jnp.split(
        proj, [FOX_WIDTH, 2 * FOX_WIDTH, 3 * FOX_WIDTH, 4 * FOX_WIDTH], axis=-1)
    shp = (bsz, seqlen, FOX_HEADS, FOX_HEAD_DIM)
    att = fox_attention(q.reshape(shp), k.reshape(shp), v.reshape(shp), f_logit, b_f)
    out = (att * jax.nn.silu(gate)).astype(x.dtype)
    return out @ w_out


def setup_inputs(seed: int = 0) -> dict:
    key = jax.random.key(seed)
    ks = jax.random.split(key, 32)
    nrm = lambda k, shp, s: jax.random.normal(k, shp, F32) * s
    x = nrm(ks[0], (BATCH, SEQ, D_MODEL), 1.0)
    l0_norm_w = 1.0 + nrm(ks[1], (D_MODEL,), 0.02)
    l0_w_in = nrm(ks[2], (D_MODEL, EVEN_IN), D_MODEL ** -0.5)
    l0_s5_lambda_re = -0.5 + nrm(ks[3], (S5_GROUPS, S5_STATE), 0.01)
    l0_s5_lambda_im = (jnp.pi * jnp.broadcast_to(jnp.arange(S5_STATE, dtype=F32), (S5_GROUPS, S5_STATE))
                       + nrm(ks[4], (S5_GROUPS, S5_STATE), 0.01))
    l0_s5_log_step = jax.random.uniform(ks[5], (S5_GROUPS,), F32, math.log(1e-3), math.log(1e-1))
    l0_s5_b_re = nrm(ks[6], (S5_GROUPS, S5_STATE, S5_GROUP), (2 * S5_GROUP) ** -0.5)
    l0_s5_b_im = nrm(ks[7], (S5_GROUPS, S5_STATE, S5_GROUP), (2 * S5_GROUP) ** -0.5)
    l0_s5_c_re = nrm(ks[8], (S5_GROUPS, S5_GROUP, S5_STATE), S5_STATE ** -0.5)
    l0_s5_c_im = nrm(ks[9], (S5_GROUPS, S5_GROUP, S5_STATE), S5_STATE ** -0.5)
    l0_s5_d = nrm(ks[10], (S5_GROUPS, S5_GROUP), 1.0)
    l0_s5_w_glu = nrm(ks[11], (S5_WIDTH, S5_WIDTH), S5_WIDTH ** -0.5)
    l0_s5_b_glu = nrm(ks[12], (S5_WIDTH,), 0.01)
    l0_ssd_conv_w = nrm(ks[13], (SSD_CONV, SSD_XBC), SSD_CONV ** -0.5)
    l0_ssd_conv_b = nrm(ks[14], (SSD_XBC,), 0.01)
    dt0 = jnp.exp(jax.random.uniform(ks[15], (SSD_HEADS,), F32, math.log(1e-3), math.log(1e-1)))
    l0_ssd_dt_bias = dt0 + jnp.log(-jnp.expm1(-dt0))
    l0_ssd_a_log = jnp.log(jax.random.uniform(ks[16], (SSD_HEADS,), F32, 1.0, 16.0))
    l0_ssd_d = 1.0 + nrm(ks[17], (SSD_HEADS,), 0.01)
    l0_ssd_norm_w = 1.0 + nrm(ks[18], (SSD_WIDTH,), 0.02)
    l0_w_out = nrm(ks[19], (MIX_WIDTH, D_MODEL), MIX_WIDTH ** -0.5)
    l1_norm_w = 1.0 + nrm(ks[20], (D_MODEL,), 0.02)
    l1_w_in = nrm(ks[21], (D_MODEL, ODD_IN), D_MODEL ** -0.5)
    l1_fox_b_f = jnp.log(jnp.exp(jax.random.uniform(ks[22], (FOX_HEADS,), F32, math.log(8.0), math.log(2048.0))))
    l1_w_out = nrm(ks[23], (FOX_WIDTH, D_MODEL), FOX_WIDTH ** -0.5)
    final_norm_w = 1.0 + nrm(ks[24], (D_MODEL,), 0.02)
    return {
        "x": x,
        "l0_norm_w": l0_norm_w, "l0_w_in": l0_w_in,
        "l0_s5_lambda_re": l0_s5_lambda_re, "l0_s5_lambda_im": l0_s5_lambda_im,
        "l0_s5_log_step": l0_s5_log_step,
        "l0_s5_b_re": l0_s5_b_re, "l0_s5_b_im": l0_s5_b_im,
        "l0_s5_c_re": l0_s5_c_re, "l0_s5_c_im": l0_s5_c_im,
        "l0_s5_d": l0_s5_d, "l0_s5_w_glu": l0_s5_w_glu, "l0_s5_b_glu": l0_s5_b_glu,
        "l0_ssd_conv_w": l0_ssd_conv_w, "l0_ssd_conv_b": l0_ssd_conv_b,
        "l0_ssd_dt_bias": l0_ssd_dt_bias, "l0_ssd_a_log": l0_ssd_a_log,
        "l0_ssd_d": l0_ssd_d, "l0_ssd_norm_w": l0_ssd_norm_w,
        "l0_w_out": l0_w_out,
        "l1_norm_w": l1_norm_w, "l1_w_in": l1_w_in, "l1_fox_b_f": l1_fox_b_f,
        "l1_w_out": l1_w_out,
        "final_norm_w": final_norm_w,
    }


def reference(x, l0_norm_w, l0_w_in, l0_s5_lambda_re, l0_s5_lambda_im, l0_s5_log_step,
              l0_s5_b_re, l0_s5_b_im, l0_s5_c_re, l0_s5_c_im, l0_s5_d, l0_s5_w_glu,
              l0_s5_b_glu, l0_ssd_conv_w, l0_ssd_conv_b, l0_ssd_dt_bias, l0_ssd_a_log,
              l0_ssd_d, l0_ssd_norm_w, l0_w_out, l1_norm_w, l1_w_in, l1_fox_b_f, l1_w_out,
              final_norm_w):
    layers = [
        (l0_norm_w, l0_w_in, l0_s5_lambda_re, l0_s5_lambda_im, l0_s5_log_step,
         l0_s5_b_re, l0_s5_b_im, l0_s5_c_re, l0_s5_c_im, l0_s5_d, l0_s5_w_glu,
         l0_s5_b_glu, l0_ssd_conv_w, l0_ssd_conv_b, l0_ssd_dt_bias, l0_ssd_a_log,
         l0_ssd_d, l0_ssd_norm_w, l0_w_out),
        (l1_norm_w, l1_w_in, l1_fox_b_f, l1_w_out),
    ]
    for layer in range(DEPTH):
        if layer % 2 == 0:
            x = x + ssm_layer(x, *layers[layer])
        else:
            x = x + fox_layer(x, *layers[layer])
    return rms_norm(x, final_norm_w)
```

```python
import contextlib
import numpy as np
import ml_dtypes
import concourse.bass as bass
import concourse.mybir as mybir
from concourse.bass_utils import run_bass_kernel_spmd

F32 = mybir.dt.float32
BF16 = mybir.dt.bfloat16
AF = mybir.ActivationFunctionType
ALU = mybir.AluOpType
NP_BF16 = ml_dtypes.bfloat16

NCORES = 8
NT = 8192
SEQ = 4096
DM = 4096
TC = 512
NCH = NT // TC
EPS = 1e-5


class Buf:
    __slots__ = ("name", "w", "r", "sem", "semcnt", "key")

    def __init__(self, name):
        self.name = name
        self.w = {}
        self.r = {}
        self.sem = None
        self.semcnt = 0
        self.key = None


class PB:
    ENG = ("pe", "act", "dve", "pool", "sp")

    def __init__(self, nc):
        self.nc = nc
        self.q = {e: [] for e in self.ENG}
        self.cnt = {}
        self.known = {e: {} for e in self.ENG}
        self.handles = {}
        self.nbuf = 0
        self.free_sems = []
        self.stage_sems = []
        for e in ("pe", "act", "dve", "pool"):
            self.handles[e] = nc.alloc_semaphore("s_" + e)
            self.cnt[e] = 0

    def buf(self, name="b"):
        return Buf(name)

    def _deps(self, reads, writes):
        deps = {}
        for b in reads:
            for k, v in b.w.items():
                if deps.get(k, 0) < v:
                    deps[k] = v
        for b in writes:
            for k, v in b.w.items():
                if deps.get(k, 0) < v:
                    deps[k] = v
            for k, v in b.r.items():
                if deps.get(k, 0) < v:
                    deps[k] = v
        return deps

    def _wait(self, eng, deps):
        kn = self.known[eng]
        for k, v in deps.items():
            if k == "pe" and eng == "pe":
                continue
            if kn.get(k, 0) >= v:
                continue
            kn[k] = v
            sem = self.handles[k]
            self.q[eng].append(lambda e, sem=sem, v=v: e.wait_ge(sem, v))

    def _mark(self, tok, reads, writes):
        k, v = tok
        for b in reads:
            b.r[k] = v
        for b in writes:
            b.w = {k: v}
            b.r = {}

    def op(self, eng, fn, reads=(), writes=()):
        self._wait(eng, self._deps(reads, writes))
        self.cnt[eng] += 1
        v = self.cnt[eng]
        sem = self.handles[eng]
        self.q[eng].append(lambda e, fn=fn, sem=sem: fn(e).then_inc(sem, 1))
        self.known[eng][eng] = max(self.known[eng].get(eng, 0), 0)
        self._mark((eng, v), reads, writes)

    def dma(self, q, out, in_, sb, reads=(), writes=(), **kw):
        self._wait(q, self._deps(reads, writes))
        if sb.sem is None:
            if self.free_sems:
                sb.key = self.free_sems.pop()
            else:
                self.nbuf += 1
                sb.key = ("d", self.nbuf)
                self.handles[sb.key] = self.nc.alloc_semaphore("d%d" % self.nbuf)
                self.cnt[sb.key] = 0
            sb.sem = self.handles[sb.key]
            self.stage_sems.append(sb.key)
        self.cnt[sb.key] += 16
        sem = sb.sem
        self.q[q].append(lambda e, out=out, in_=in_, sem=sem, kw=kw:
                         e.dma_start(out=out, in_=in_, **kw).then_inc(sem, 16))
        self._mark((sb.key, self.cnt[sb.key]), reads, writes)

    def barrier(self):
        for eng in self.ENG:
            self._wait(eng, dict(self.cnt))

    def flush(self):
        nc = self.nc
        q = self.q
        with nc.Block() as block:
            @block.tensor
            def _(e):
                for f in q["pe"]:
                    f(e)

            @block.scalar
            def _(e):
                for f in q["act"]:
                    f(e)

            @block.vector
            def _(e):
                for f in q["dve"]:
                    f(e)

            @block.gpsimd
            def _(e):
                for f in q["pool"]:
                    f(e)

            @block.sync
            def _(e):
                for f in q["sp"]:
                    f(e)
        self.q = {e: [] for e in self.ENG}
        self.free_sems.extend(self.stage_sems)
        self.stage_sems = []

    def allgather(self, src, dst):
        self.barrier()
        key = ("cc",)
        if key not in self.handles:
            self.handles[key] = self.nc.alloc_semaphore("cc")
            self.cnt[key] = 0
        self.cnt[key] += 16
        sem = self.handles[key]
        groups = [list(range(NCORES))]
        self.q["pool"].append(lambda e: e.collective_compute(
            "AllGather", ALU.bypass, replica_groups=groups, ins=[src], outs=[dst]).then_inc(sem, 16))
        self.barrier()


class Ctx:
    def __init__(self, pb, es):
        self.pb = pb
        self.nc = pb.nc
        self.es = es
        self.n = 0

    def sb(self, shape, dt, name="t"):
        self.pb.nbuf += 1
        self.n = self.pb.nbuf
        t = self.es.enter_context(self.nc.sbuf_tensor("%s_%d" % (name, self.n), list(shape), dt))
        return t, self.pb.buf(name)

    def ps(self, shape, dt=F32, name="p"):
        self.pb.nbuf += 1
        self.n = self.pb.nbuf
        t = self.es.enter_context(self.nc.psum_tensor("%s_%d" % (name, self.n), list(shape), dt))
        return t, self.pb.buf(name)


def stage_ssq(pb, es, xs, ssq_out):
    c = Ctx(pb, es)
    xt = [c.sb([128, 4, TC], F32, "xt") for _ in range(2)]
    sq = [c.sb([128, 4, TC], F32, "sq") for _ in range(2)]
    ones, ones_b = c.sb([128, 1], F32, "ones")
    row, row_b = c.sb([1, NT], F32, "row")
    ps = [c.ps([1, TC], F32, "ps") for _ in range(2)]
    pb.op("dve", lambda e: e.memset(ones[:], 1.0), writes=[ones_b])
    xv = xs.rearrange("(f p) n -> p f n", p=128)
    for ch in range(NCH):
        t, tb = xt[ch % 2]
        s, sbf = sq[ch % 2]
        p, pbf = ps[ch % 2]
        sl = slice(ch * TC, (ch + 1) * TC)
        pb.dma("sp", t[:], xv[:, :, sl], tb, writes=[tb])
        pb.op("act", lambda e, s=s, t=t: e.activation(s[:], t[:], AF.Square), reads=[tb], writes=[sbf])
        for f in range(4):
            pb.op("pe", lambda e, p=p, s=s, f=f: e.matmul(p[:], ones[:], s[:, f, :], start=(f == 0), stop=(f == 3)),
                  reads=[sbf, ones_b], writes=[pbf])
        pb.op("dve", lambda e, p=p, sl=sl: e.tensor_copy(row[0:1, sl], p[:]), reads=[pbf], writes=[row_b])
    pb.dma("sp", ssq_out, row[:], row_b, reads=[row_b])


def stage_apply(pb, es, xs, ssq_all, nw_d, out, out_dt):
    c = Ctx(pb, es)
    xt = [c.sb([128, 4, TC], F32, "xt") for _ in range(2)]
    ot = [c.sb([128, 4, TC], out_dt, "ot") for _ in range(2)]
    pt = [c.sb([8, TC], F32, "pt") for _ in range(2)]
    rs = [c.sb([128, TC], F32, "rs") for _ in range(2)]
    ones, ones_b = c.sb([8, 128], F32, "ones")
    nw, nw_b = c.sb([128, 4], F32, "nw")
    ps = [c.ps([128, TC], F32, "ps") for _ in range(2)]
    pb.op("dve", lambda e: e.memset(ones[:], 1.0), writes=[ones_b])
    pb.dma("sp", nw[:], nw_d, nw_b, writes=[nw_b])
    xv = xs.rearrange("(f p) n -> p f n", p=128)
    ov = out.rearrange("(f p) n -> p f n", p=128)
    for ch in range(NCH):
        t, tb = xt[ch % 2]
        o, ob = ot[ch % 2]
        q, qb = pt[ch % 2]
        r, rb = rs[ch % 2]
        p, pbf = ps[ch % 2]
        sl = slice(ch * TC, (ch + 1) * TC)
        pb.dma("sp", t[:], xv[:, :, sl], tb, writes=[tb])
        pb.dma("sp", q[:], ssq_all[:, sl], qb, writes=[qb])
        pb.op("pe", lambda e, p=p, q=q: e.matmul(p[:], ones[:], q[:], start=True, stop=True),
              reads=[qb, ones_b], writes=[pbf])
        pb.op("dve", lambda e, r=r, p=p: e.tensor_scalar(r[:], p[:], 1.0 / DM, EPS, ALU.mult, ALU.add),
              reads=[pbf], writes=[rb])
        pb.op("act", lambda e, r=r: e.activation(r[:], r[:], AF.Sqrt), reads=[rb], writes=[rb])
        pb.op("dve", lambda e, r=r: e.reciprocal(r[:], r[:]), reads=[rb], writes=[rb])
        for f in range(4):
            pb.op("dve", lambda e, o=o, t=t, r=r, f=f: e.scalar_tensor_tensor(
                o[:, f, :], t[:, f, :], nw[:, f:f + 1], r[:], ALU.mult, ALU.mult),
                reads=[tb, rb, nw_b], writes=[ob])
        pb.dma("pool", ov[:, :, sl], o[:], ob, reads=[ob])


def stage_linear(pb, es, act, K, groups):
    c = Ctx(pb, es)
    KC = K // 128
    RK = 8
    NR = 6
    nW = 2 if KC <= 32 else 1
    wall, _ = c.sb([128, nW, KC, 512], BF16, "W")
    wbufs = [pb.buf("W%d" % i) for i in range(nW)]
    RS = 4
    stg = [c.sb([128, RS, 512], F32, "stg") for _ in range(2)]
    ring = [c.sb([128, RK, TC], BF16, "ring") for _ in range(NR)]
    outt = [c.sb([128, 4, 512], F32, "outt") for _ in range(2)]
    outb = [c.sb([128, 4, 512], BF16, "outb") for _ in range(2)] if any(g.get("dt") == BF16 for g in groups) else None
    need_ssq = any(g.get("ssq") is not None for g in groups)
    nset = 1 if need_ssq else 2
    psets = [[c.ps([128, 512], F32, "acc") for _ in range(4)] for _ in range(nset)]
    if need_ssq:
        rst = [c.sb([128, 4, TC], F32, "rst") for _ in range(2)]
        sqt = [c.sb([128, 4, TC], F32, "sqt") for _ in range(2)]
        ones, ones_b = c.sb([128, 1], F32, "ones")
        rows = [c.sb([1, TC], F32, "row") for _ in range(2)]
        pss = c.ps([1, TC], F32, "pss")
        pb.op("dve", lambda e: e.memset(ones[:], 1.0), writes=[ones_b])
    av = act.rearrange("(k p) n -> p k n", p=128)

    st_i = [0]
    cast_i = [0]

    def load_w(gi):
        g = groups[gi]
        fn = g["fn"]
        wi = gi % nW
        wb = wbufs[wi]
        for k0 in range(0, KC, RS):
            s, sbuf = stg[st_i[0] % 2]
            st_i[0] += 1
            pb.dma("sp", s[:, :, 0:fn], g["w"][:, k0:k0 + RS, :], sbuf, writes=[sbuf])
            eng = ("pool", "act")[cast_i[0] % 2]
            cast_i[0] += 1
            if eng == "pool":
                pb.op("pool", lambda e, s=s, wi=wi, k0=k0, fn=fn: e.tensor_copy(
                    wall[:, wi, k0:k0 + RS, 0:fn], s[:, :, 0:fn]), reads=[sbuf], writes=[wb])
            else:
                pb.op("act", lambda e, s=s, wi=wi, k0=k0, fn=fn: e.activation(
                    wall[:, wi, k0:k0 + RS, 0:fn], s[:, :, 0:fn], AF.Copy), reads=[sbuf], writes=[wb])

    ring_i = [0]
    chunk_i = [0]
    load_w(0)
    for gi, g in enumerate(groups):
        fn = g["fn"]
        fm = g["orient"] == "fm"
        wi = gi % nW
        wb = wbufs[wi]
        nft = (fn + 127) // 128
        if nW == 2 and gi + 1 < len(groups):
            load_w(gi + 1)
        for ch in range(NCH):
            sl = slice(ch * TC, (ch + 1) * TC)
            pset = psets[chunk_i[0] % nset]
            o, ob = (outb if g.get("dt") == BF16 else outt)[chunk_i[0] % 2]
            chunk_i[0] += 1
            nacc = nft if fm else 4
            for k0 in range(0, KC, RK):
                a, ab = ring[ring_i[0] % NR]
                ring_i[0] += 1
                pb.dma("sp", a[:], av[:, k0:k0 + RK, sl], ab, writes=[ab])
                for j in range(nacc):
                    p, pbf = pset[j]
                    for kk in range(RK):
                        k = k0 + kk
                        if fm:
                            m = min(128, fn - j * 128)
                            pb.op("pe", lambda e, p=p, a=a, kk=kk, k=k, j=j, m=m, wi=wi: e.matmul(
                                p[0:m, :], wall[:, wi, k, j * 128:j * 128 + m], a[:, kk, :],
                                start=(k == 0), stop=(k == KC - 1)), reads=[ab, wb], writes=[pbf])
                        else:
                            pb.op("pe", lambda e, p=p, a=a, kk=kk, k=k, j=j, fn=fn, wi=wi: e.matmul(
                                p[:, 0:fn], a[:, kk, j * 128:(j + 1) * 128], wall[:, wi, k, 0:fn],
                                start=(k == 0), stop=(k == KC - 1)), reads=[ab, wb], writes=[pbf])
            resid = g.get("resid")
            if resid is not None:
                r, rb = rst[ch % 2]
                s, sbf = sqt[ch % 2]
                pb.dma("sp", r[:], resid.rearrange("(f p) n -> p f n", p=128)[:, :, sl], rb, writes=[rb])
                for j in range(4):
                    p, pbf = pset[j]
                    pb.op("dve", lambda e, o=o, p=p, r=r, j=j: e.tensor_tensor(o[:, j, :], p[:], r[:, j, :], ALU.add),
                          reads=[pbf, rb], writes=[ob])
                pb.op("act", lambda e, s=s, o=o: e.activation(s[:], o[:], AF.Square), reads=[ob], writes=[sbf])
                for j in range(4):
                    pb.op("pe", lambda e, s=s, j=j: e.matmul(pss[0][:], ones[:], s[:, j, :], start=(j == 0), stop=(j == 3)),
                          reads=[sbf, ones_b], writes=[pss[1]])
                row, row_b = rows[ch % 2]
                pb.op("dve", lambda e, row=row: e.tensor_copy(row[:], pss[0][:]), reads=[pss[1]], writes=[row_b])
                pb.dma("pool", g["ssq"][0:1, sl], row[:], row_b, reads=[row_b])
                pb.dma("pool", g["dst"].rearrange("(f p) n -> p f n", p=128)[:, :, sl], o[:], ob, reads=[ob])
            elif fm:
                for j in range(nft):
                    p, pbf = pset[j]
                    m = min(128, fn - j * 128)
                    if j % 2 == 0:
                        pb.op("dve", lambda e, o=o, p=p, j=j, m=m: e.tensor_copy(o[0:m, j, :], p[0:m, :]),
                              reads=[pbf], writes=[ob])
                    else:
                        pb.op("act", lambda e, o=o, p=p, j=j, m=m: e.activation(o[0:m, j, :], p[0:m, :], AF.Copy),
                              reads=[pbf], writes=[ob])
                if fn % 128 == 0:
                    pb.dma("pool", g["dst"].rearrange("(f p) n -> p f n", p=128)[:, :, sl], o[:, 0:nft, :], ob, reads=[ob])
                else:
                    assert nft == 1
                    pb.dma("pool", g["dst"][:, sl], o[0:fn, 0, :], ob, reads=[ob])
            else:
                for j in range(4):
                    p, pbf = pset[j]
                    if j % 2 == 0:
                        pb.op("dve", lambda e, o=o, p=p, j=j, fn=fn: e.tensor_copy(o[:, j, 0:fn], p[:, 0:fn]),
                              reads=[pbf], writes=[ob])
                    else:
                        pb.op("act", lambda e, o=o, p=p, j=j, fn=fn: e.activation(o[:, j, 0:fn], p[:, 0:fn], AF.Copy),
                              reads=[pbf], writes=[ob])
                pb.dma("pool", g["dst"][sl, :].rearrange("(t p) f -> p t f", p=128), o[:, :, 0:fn], ob, reads=[ob])


def new_nc():
    return bass.Bass("TRN2", target_bir_lowering=False)


def run_stages(nc, pb_stages, in_maps):
    res = run_bass_kernel_spmd(nc, in_maps, core_ids=list(range(NCORES)))
    return res.results


def build_and_run(stage_fns, in_specs, out_specs, in_maps):
    nc = new_nc()
    T = {}
    for n, (shp, dt) in in_specs.items():
        T[n] = nc.dram_tensor(n, list(shp), dt, kind="ExternalInput").ap()
    for n, (shp, dt) in out_specs.items():
        T[n] = nc.dram_tensor(n, list(shp), dt, kind="ExternalOutput").ap()
    pb = PB(nc)
    for fn in stage_fns:
        with contextlib.ExitStack() as es:
            fn(pb, es, T)
            pb.barrier()
            pb.flush()
    res = run_bass_kernel_spmd(nc, in_maps, core_ids=list(range(NCORES)))
    return res.results


def make_consts(pb, c):
    d, db = c.sb([128, 128], F32, "iota")
    ident, ident_b = c.sb([128, 128], F32, "ident")
    trif, trif_b = c.sb([128, 128], F32, "trif")
    tri, tri_b = c.sb([128, 128], BF16, "tri")
    identb, identb_b = c.sb([128, 128], BF16, "identb")
    pb.op("pool", lambda e: e.iota(d[:], [[1, 128]], base=0, channel_multiplier=-1,
                                   allow_small_or_imprecise_dtypes=True), writes=[db])
    pb.op("dve", lambda e: e.tensor_single_scalar(ident[:], d[:], 0.0, ALU.is_equal), reads=[db], writes=[ident_b])
    pb.op("dve", lambda e: e.tensor_single_scalar(trif[:], d[:], 0.0, ALU.is_ge), reads=[db], writes=[trif_b])
    pb.op("dve", lambda e: e.tensor_copy(tri[:], trif[:]), reads=[trif_b], writes=[tri_b])
    pb.op("dve", lambda e: e.tensor_copy(identb[:], ident[:]), reads=[ident_b], writes=[identb_b])
    return dict(ident=(ident, ident_b), trif=(trif, trif_b), tri=(tri, tri_b), identb=(identb, identb_b))


def stage_fox(pb, es, qT, kT, v, gate, f, bf_d, att_out):
    c = Ctx(pb, es)
    K = make_consts(pb, c)
    ident, ident_b = K["ident"]
    tri, tri_b = K["tri"]
    SC = 1.0 / np.sqrt(128.0)
    NKT = SEQ // 128
    fsb, fsb_b = c.sb([4, SEQ], F32, "fsb")
    csb, csb_b = c.sb([4, SEQ], F32, "csb")
    ones4, ones4_b = c.sb([4, SEQ], F32, "ones4")
    negbf, negbf_b = c.sb([4, 1], F32, "negbf")
    negc, negc_b = c.sb([128, NKT, 4], F32, "negc")
    cm = [c.sb([4, TC], F32, "cm") for _ in range(2)]
    cqs = [c.sb([128, TC], F32, "cqs") for _ in range(2)]
    QT = [c.sb([128, SEQ], BF16, "QT") for _ in range(2)]
    KT = [c.sb([128, SEQ], BF16, "KT") for _ in range(2)]
    VX = [c.sb([128, NKT, 132], BF16, "VX") for _ in range(2)]
    GT = [c.sb([128, NKT, 128], F32, "GT") for _ in range(2)]
    tsb = [c.sb([128, TC], F32, "tsb") for _ in range(2)]
    ptb = [c.sb([128, TC], BF16, "ptb") for _ in range(3)]
    rl = [c.sb([128, 1], F32, "rl") for _ in range(2)]
    ge = [c.sb([128, 128], F32, "ge") for _ in range(2)]
    of = [c.sb([128, 128], F32, "of") for _ in range(2)]
    osb = [c.sb([128, TC], BF16, "osb") for _ in range(2)]
    ps_st = [c.ps([128, TC], F32, "pst") for _ in range(2)]
    ps_o = [c.ps([128, 512], F32, "pso") for _ in range(4)]
    ps_cq = c.ps([128, TC], F32, "pcq")
    ps_m = c.ps([128, 512], F32, "pm")

    pb.op("dve", lambda e: e.memset(ones4[:], 1.0), writes=[ones4_b])
    pb.dma("sp", negbf[:], bf_d, negbf_b, writes=[negbf_b])
    pb.op("dve", lambda e: e.tensor_scalar(negbf[:], negbf[:], -1.0, None, ALU.mult), reads=[negbf_b], writes=[negbf_b])
    for t, tb in VX:
        pb.op("pool", lambda e, t=t: e.memset(t[:, :, 128:132], 1.0), writes=[tb])

    it = 0
    hq = 0
    for b in range(2):
        t0 = b * SEQ
        pb.dma("sp", fsb[:], f[:, t0:t0 + SEQ], fsb_b, writes=[fsb_b])
        pb.op("act", lambda e: e.activation(fsb[:], fsb[:], AF.Exp, bias=negbf[:, 0:1], scale=-1.0),
              reads=[fsb_b, negbf_b], writes=[fsb_b])
        pb.op("act", lambda e: e.activation(fsb[:], fsb[:], AF.Ln, bias=1.0), reads=[fsb_b], writes=[fsb_b])
        pb.op("dve", lambda e: e.tensor_tensor_scan(csb[:], ones4[:], fsb[:], 0.0, ALU.mult, ALU.subtract),
              reads=[ones4_b, fsb_b], writes=[csb_b])
        for j in range(NKT):
            pb.op("pe", lambda e, j=j: e.transpose(ps_m[0][:, 4 * j:4 * j + 4], csb[0:4, j * 128:(j + 1) * 128], ident[0:4, 0:4]),
                  reads=[csb_b, ident_b], writes=[ps_m[1]])
        pb.op("act", lambda e: e.activation(negc[:].rearrange("p a b -> p (a b)"), ps_m[0][:, 0:NKT * 4], AF.Copy, scale=-1.0),
              reads=[ps_m[1]], writes=[negc_b])
        for h in range(4):
            qt_, qb = QT[hq % 2]
            kt_, kb = KT[hq % 2]
            vx, vb = VX[hq % 2]
            gt, gb = GT[hq % 2]
            hq += 1
            hs = slice(h * 128, (h + 1) * 128)
            pb.dma("sp", qt_[:], qT[hs, t0:t0 + SEQ], qb, writes=[qb])
            pb.dma("sp", kt_[:], kT[hs, t0:t0 + SEQ], kb, writes=[kb])
            pb.dma("sp", vx[:, :, 0:128], v[t0:t0 + SEQ, hs].rearrange("(j p) d -> p j d", p=128), vb, writes=[vb])
            pb.dma("sp", gt[:], gate[t0:t0 + SEQ, hs].rearrange("(j p) d -> p j d", p=128), gb, writes=[gb])
            for Q in range(SEQ // TC):
                cmt, cmb = cm[Q % 2]
                cq, cqb = cqs[Q % 2]
                qsl = slice(Q * TC, (Q + 1) * TC)
                pb.op("dve", lambda e, cmt=cmt, qsl=qsl, h=h: e.tensor_scalar(
                    cmt[:], csb[:, qsl], ident[0:4, h:h + 1], None, ALU.mult), reads=[csb_b, ident_b], writes=[cmb])
                pb.op("pe", lambda e, cmt=cmt: e.matmul(ps_cq[0][:], ones4[0:4, 0:128], cmt[:], start=True, stop=True),
                      reads=[cmb, ones4_b], writes=[ps_cq[1]])
                pb.op("act", lambda e, cq=cq: e.activation(cq[:], ps_cq[0][:], AF.Copy), reads=[ps_cq[1]], writes=[cqb])
                nkt = 4 * Q + 4
                for kt in range(nkt):
                    j = kt - 4 * Q
                    col0 = max(0, j) * 128
                    st, stb = ps_st[it % 2]
                    ts, tsbb = tsb[it % 2]
                    pt, ptbb = ptb[it % 3]
                    it += 1
                    pb.op("pe", lambda e, st=st, kt_=kt_, qt_=qt_, kt=kt, Q=Q, col0=col0: e.matmul(
                        st[:, col0:TC], kt_[:, kt * 128:(kt + 1) * 128], qt_[:, Q * TC + col0:(Q + 1) * TC],
                        start=True, stop=True), reads=[kb, qb], writes=[stb])
                    pb.op("dve", lambda e, ts=ts, st=st, cq=cq, col0=col0: e.scalar_tensor_tensor(
                        ts[:, col0:TC], st[:, col0:TC], float(SC), cq[:, col0:TC], ALU.mult, ALU.add),
                        reads=[stb, cqb], writes=[tsbb])
                    pb.op("act", lambda e, pt=pt, ts=ts, kt=kt, h=h, col0=col0: e.activation(
                        pt[:, col0:TC], ts[:, col0:TC], AF.Exp, bias=negc[:, kt, h:h + 1], scale=1.0),
                        reads=[tsbb, negc_b], writes=[ptbb])
                    if j >= 0:
                        pb.op("pool", lambda e, pt=pt, col0=col0: e.tensor_tensor(
                            pt[:, col0:col0 + 128], pt[:, col0:col0 + 128], tri[:], ALU.mult),
                            reads=[ptbb, tri_b], writes=[ptbb])
                    for qi in range(max(0, j), 4):
                        po, pob = ps_o[qi]
                        pb.op("pe", lambda e, po=po, pt=pt, vx=vx, kt=kt, qi=qi, Q=Q: e.matmul(
                            po[:, 0:129], pt[:, qi * 128:(qi + 1) * 128], vx[:, kt, 0:129],
                            start=(kt == 0), stop=(kt == 4 * Q + qi)), reads=[ptbb, vb], writes=[pob])
                ob_, obb = osb[Q % 2]
                for qi in range(4):
                    po, pob = ps_o[qi]
                    r, rb = rl[qi % 2]
                    g, gbb = ge[qi % 2]
                    o, ofb = of[qi % 2]
                    jj = Q * 4 + qi
                    pb.op("dve", lambda e, r=r, po=po: e.reciprocal(r[:], po[:, 128:129]), reads=[pob], writes=[rb])
                    pb.op("act", lambda e, g=g, gt=gt, jj=jj: e.activation(g[:], gt[:, jj, :], AF.Exp, scale=-1.0),
                          reads=[gb], writes=[gbb])
                    pb.op("dve", lambda e, g=g: e.tensor_scalar(g[:], g[:], 1.0, None, ALU.add), reads=[gbb], writes=[gbb])
                    pb.op("dve", lambda e, g=g: e.reciprocal(g[:], g[:]), reads=[gbb], writes=[gbb])
                    pb.op("dve", lambda e, g=g, gt=gt, jj=jj: e.tensor_tensor(g[:], g[:], gt[:, jj, :], ALU.mult),
                          reads=[gbb, gb], writes=[gbb])
                    pb.op("dve", lambda e, o=o, po=po, r=r, g=g: e.scalar_tensor_tensor(
                        o[:], po[:, 0:128], r[:, 0:1], g[:], ALU.mult, ALU.mult), reads=[pob, rb, gbb], writes=[ofb])
                    pb.op("pe", lambda e, o=o, qi=qi: e.transpose(ps_m[0][:, qi * 128:(qi + 1) * 128], o[:], ident[:]),
                          reads=[ofb, ident_b], writes=[ps_m[1]])
                pb.op("act", lambda e, ob_=ob_: e.activation(ob_[:], ps_m[0][:], AF.Copy), reads=[ps_m[1]], writes=[obb])
                pb.dma("pool", att_out[hs, t0 + Q * TC:t0 + (Q + 1) * TC], ob_[:], obb, reads=[obb])


def wlay(w):
    K, fn = w.shape
    return np.ascontiguousarray(w.reshape(K // 128, 128, fn).transpose(1, 0, 2))


def dram(nc, name, shape, dt, kind="Internal"):
    return nc.dram_tensor(name, list(shape), dt, kind=kind).ap()


def launch(build, in_maps):
    nc = new_nc()
    pb = PB(nc)
    build(nc, pb)
    res = run_bass_kernel_spmd(nc, in_maps, core_ids=list(range(NCORES)))
    return res.results


def run_stage(pb, fn, *args):
    with contextlib.ExitStack() as es:
        fn(pb, es, *args)
        pb.barrier()
        pb.flush()


def l1_inmaps(xn_all, w_in, b_f):
    maps = []
    FW = 4096
    for i in range(NCORES):
        cs = slice(512 * i, 512 * (i + 1))
        m = {"act": xn_all}
        for n, off in (("wq", 0), ("wk", FW), ("wv", 2 * FW), ("wg", 3 * FW)):
            m[n] = wlay(w_in[:, off + 512 * i: off + 512 * (i + 1)])
        m["wf"] = wlay(w_in[:, 4 * FW + 4 * i: 4 * FW + 4 * (i + 1)])
        m["bf"] = np.ascontiguousarray(b_f[4 * i:4 * i + 4].reshape(4, 1))
        maps.append(m)
    return maps


def build_l1(nc, pb):
    act = dram(nc, "act", (DM, NT), BF16, "ExternalInput")
    W = {n: dram(nc, n, (128, 32, 512), F32, "ExternalInput") for n in ("wq", "wk", "wv", "wg")}
    wf = dram(nc, "wf", (128, 32, 4), F32, "ExternalInput")
    bf = dram(nc, "bf", (4, 1), F32, "ExternalInput")
    att = dram(nc, "att", (512, NT), BF16, "ExternalOutput")
    q = dram(nc, "q_i", (512, NT), BF16)
    k = dram(nc, "k_i", (512, NT), BF16)
    v = dram(nc, "v_i", (NT, 512), BF16)
    g = dram(nc, "g_i", (NT, 512), F32)
    f = dram(nc, "f_i", (4, NT), F32)
    groups = [
        dict(w=W["wq"], fn=512, orient="fm", dst=q, dt=BF16),
        dict(w=W["wk"], fn=512, orient="fm", dst=k, dt=BF16),
        dict(w=W["wv"], fn=512, orient="tm", dst=v, dt=BF16),
        dict(w=W["wg"], fn=512, orient="tm", dst=g),
        dict(w=wf, fn=4, orient="fm", dst=f),
    ]
    run_stage(pb, stage_linear, act, DM, groups)
    run_stage(pb, stage_fox, q, k, v, g, f, bf, att)


def stage_ssd(pb, es, ztm, xbc, cw_d, cb_d, dtb_d, alog_d, dd_d, nw_d, out):
    c = Ctx(pb, es)
    K = make_consts(pb, c)
    ident, ident_b = K["ident"]
    trif, trif_b = K["trif"]
    identb, identb_b = K["identb"]
    H = 12
    cw, cw_b = c.sb([128, 8, 4], F32, "cw")
    cb, cb_b = c.sb([128, 8], F32, "cb")
    dtb, dtb_b = c.sb([128, H], F32, "dtb")
    nega, nega_b = c.sb([128, H], F32, "nega")
    dd, dd_b = c.sb([128, H], F32, "dd")
    nw, nw_b = c.sb([128, 768], F32, "nw")
    mneg, mneg_b = c.sb([128, 128], BF16, "mneg")
    raw = [c.sb([128, 8, 3 + TC], F32, "raw") for _ in range(2)]
    acc, acc_b = c.sb([128, 8, TC], F32, "acc")
    xf = [c.sb([128, 6, TC], F32, "xf") for _ in range(2)]
    bfm = [c.sb([128, TC], F32, "bfm") for _ in range(2)]
    bbf = [c.sb([128, TC], BF16, "bbf") for _ in range(2)]
    cbf = [c.sb([128, TC], BF16, "cbf") for _ in range(2)]
    zt = [c.sb([128, 780], F32, "zt") for _ in range(2)]
    dt_, dt_b = c.sb([128, H], F32, "dt")
    la, la_b = c.sb([128, H], F32, "la")
    lc, lc_b = c.sb([128, H], F32, "lc")
    elc, elc_b = c.sb([128, H], F32, "elc")
    eend, eend_b = c.sb([128, H], F32, "eend")
    lct, lct_b = c.sb([H, 128], F32, "lct")
    rall, rall_b = c.sb([H, H, 128], F32, "rall")
    ones12, ones12_b = c.sb([H, 128], F32, "ones12")
    seg, seg_b = c.sb([128, H, 128], F32, "seg")
    dec, dec_b = c.sb([128, H, 128], F32, "dec")
    win, win_b = c.sb([128, H, 128], BF16, "win")
    sct, sct_b = c.sb([128, 128], F32, "sct")
    btm, btm_b = c.sb([128, 128], BF16, "btm")
    xtm, xtm_b = c.sb([128, H, 64], F32, "xtm")
    xc, xc_b = c.sb([128, H, 64], BF16, "xc")
    xcd, xcd_b = c.sb([128, H, 64], BF16, "xcd")
    xd, xd_b = c.sb([128, H, 64], F32, "xd")
    sf, sf_b = c.sb([128, H, 64], F32, "sf")
    sbf, sbf_b = c.sb([128, H, 64], BF16, "sbf")
    y1, y1_b = c.sb([128, H, 64], F32, "y1")
    sz, sz_b = c.sb([128, 768], F32, "sz")
    junk, junk_b = c.sb([128, 768], F32, "junk")
    ssq, ssq_b = c.sb([128, 1], F32, "ssq")
    yo = [c.sb([128, 768], F32, "yo") for _ in range(2)]
    ofm = [c.sb([128, 6, 128], BF16, "ofm") for _ in range(2)]
    P1 = c.ps([128, 1536], F32, "P1")
    P2 = c.ps([128, 1024], F32, "P2")
    P3 = c.ps([128, 1024], F32, "P3")
    P4 = c.ps([128, 512], F32, "P4")

    for t, b_, d_ in ((cw, cw_b, cw_d), (cb, cb_b, cb_d), (dtb, dtb_b, dtb_d), (nega, nega_b, alog_d),
                      (dd, dd_b, dd_d), (nw, nw_b, nw_d)):
        pb.dma("sp", t[:], d_, b_, writes=[b_])
    pb.op("act", lambda e: e.activation(nega[:], nega[:], AF.Exp), reads=[nega_b], writes=[nega_b])
    pb.op("dve", lambda e: e.tensor_scalar(nega[:], nega[:], -1.0, None, ALU.mult), reads=[nega_b], writes=[nega_b])
    pb.op("dve", lambda e: e.tensor_scalar(mneg[:], trif[:], -1.0, 30000.0, ALU.add, ALU.mult), reads=[trif_b], writes=[mneg_b])
    pb.op("dve", lambda e: e.memset(ones12[:], 1.0), writes=[ones12_b])

    xv = xbc.rearrange("(j p) n -> p j n", p=128)
    ov = out.rearrange("(j p) n -> p j n", p=128)
    blk = 0
    for b in range(2):
        for r_, rb_ in raw:
            pb.op("pool", lambda e, r_=r_: e.memset(r_[:, :, 0:3], 0.0), writes=[rb_])
        pb.op("dve", lambda e: e.memset(sf[:], 0.0), writes=[sf_b])
        pb.op("dve", lambda e: e.memset(sbf[:], 0.0), writes=[sbf_b])
        for cb512 in range(SEQ // TC):
            t0 = b * SEQ + cb512 * TC
            rw, rwb = raw[blk % 2]
            rw2, rwb2 = raw[(blk + 1) % 2]
            xfm, xfb = xf[blk % 2]
            bf_, bfb = bfm[blk % 2]
            bb_, bbb = bbf[blk % 2]
            cc_, ccb = cbf[blk % 2]
            blk += 1
            pb.dma("sp", rw[:, :, 3:3 + TC], xv[:, :, t0:t0 + TC], rwb, writes=[rwb])
            for j in range(8):
                pb.op("pool", lambda e, j=j, rw=rw: e.tensor_scalar(
                    acc[:, j, :], rw[:, j, 3:3 + TC], cw[:, j, 3:4], cb[:, j:j + 1], ALU.mult, ALU.add),
                    reads=[rwb, cw_b, cb_b], writes=[acc_b])
                for kk in range(3):
                    pb.op("dve", lambda e, j=j, rw=rw, kk=kk: e.scalar_tensor_tensor(
                        acc[:, j, :], rw[:, j, kk:kk + TC], cw[:, j, kk:kk + 1], acc[:, j, :], ALU.mult, ALU.add),
                        reads=[rwb, cw_b, acc_b], writes=[acc_b])
            pb.op("pool", lambda e, rw=rw, rw2=rw2: e.tensor_copy(rw2[:, :, 0:3], rw[:, :, TC:TC + 3]),
                  reads=[rwb], writes=[rwb2])
            pb.op("act", lambda e, xfm=xfm: e.activation(xfm[:], acc[:, 0:6, :], AF.Silu), reads=[acc_b], writes=[xfb])
            pb.op("act", lambda e, bf_=bf_: e.activation(bf_[:], acc[:, 6, :], AF.Silu), reads=[acc_b], writes=[bfb])
            pb.op("act", lambda e, cc_=cc_: e.activation(cc_[:], acc[:, 7, :], AF.Silu), reads=[acc_b], writes=[ccb])
            pb.op("pool", lambda e, bb_=bb_, bf_=bf_: e.tensor_copy(bb_[:], bf_[:]), reads=[bfb], writes=[bbb])
            for s4 in range(4):
                ts = slice(s4 * 128, (s4 + 1) * 128)
                tt0 = t0 + s4 * 128
                z, zb = zt[s4 % 2]
                yot, yob = yo[s4 % 2]
                oft, ofb = ofm[s4 % 2]
                pb.dma("sp", z[:], ztm[tt0:tt0 + 128, :], zb, writes=[zb])
                pb.op("dve", lambda e, z=z: e.tensor_tensor(dt_[:], z[:, 768:780], dtb[:], ALU.add), reads=[zb, dtb_b], writes=[dt_b])
                pb.op("act", lambda e: e.activation(dt_[:], dt_[:], AF.Exp), reads=[dt_b], writes=[dt_b])
                pb.op("act", lambda e: e.activation(dt_[:], dt_[:], AF.Ln, bias=1.0), reads=[dt_b], writes=[dt_b])
                pb.op("dve", lambda e: e.tensor_tensor(la[:], dt_[:], nega[:], ALU.mult), reads=[dt_b, nega_b], writes=[la_b])
                pb.op("pe", lambda e: e.matmul(P4[0][:, 0:H], trif[:], la[:], start=True, stop=True),
                      reads=[trif_b, la_b], writes=[P4[1]])
                pb.op("dve", lambda e: e.tensor_copy(lc[:], P4[0][:, 0:H]), reads=[P4[1]], writes=[lc_b])
                pb.op("act", lambda e: e.activation(elc[:], lc[:], AF.Exp), reads=[lc_b], writes=[elc_b])
                pb.op("pe", lambda e: e.transpose(P4[0][0:H, 128:256], lc[:], ident[:]), reads=[lc_b, ident_b], writes=[P4[1]])
                pb.op("dve", lambda e: e.tensor_copy(lct[:], P4[0][0:H, 128:256]), reads=[P4[1]], writes=[lct_b])
                pb.op("dve", lambda e: e.tensor_tensor(
                    rall[:], lct[:].unsqueeze(1).to_broadcast([H, H, 128]),
                    ident[0:H, 0:H].unsqueeze(2).to_broadcast([H, H, 128]), ALU.mult),
                    reads=[lct_b, ident_b], writes=[rall_b])
                for m3 in range(3):
                    pb.op("pe", lambda e, m3=m3: e.matmul(
                        P1[0][:, m3 * 512:(m3 + 1) * 512], ones12[:], rall[:, 4 * m3:4 * m3 + 4, :].rearrange("p a b -> p (a b)"),
                        start=True, stop=False), reads=[rall_b, ones12_b], writes=[P1[1]])
                    pb.op("pe", lambda e, m3=m3: e.matmul(
                        P1[0][:, m3 * 512:(m3 + 1) * 512], identb[:], mneg[:].unsqueeze(1).to_broadcast([128, 4, 128]),
                        start=False, stop=True), reads=[mneg_b, identb_b], writes=[P1[1]])
                p1v = P1[0][:].rearrange("p (h q) -> p h q", h=H)
                pb.op("act", lambda e: e.activation(eend[:], P1[0][:].rearrange("p (h q) -> p h q", h=H)[:, :, 127], AF.Exp),
                      reads=[P1[1]], writes=[eend_b])
                pb.op("dve", lambda e: e.tensor_tensor(
                    seg[:], P1[0][:].rearrange("p (h q) -> p h q", h=H), lc[:].unsqueeze(2).to_broadcast([128, H, 128]), ALU.subtract),
                    reads=[P1[1], lc_b], writes=[seg_b])
                pb.op("act", lambda e: e.activation(dec[:], seg[:], AF.Exp), reads=[seg_b], writes=[dec_b])
                pb.op("pe", lambda e, bb_=bb_, cc_=cc_, ts=ts: e.matmul(P4[0][:, 256:384], bb_[:, ts], cc_[:, ts], start=True, stop=True),
                      reads=[bbb, ccb], writes=[P4[1]])
                pb.op("act", lambda e: e.activation(sct[:], P4[0][:, 256:384], AF.Copy), reads=[P4[1]], writes=[sct_b])
                pb.op("dve", lambda e: e.tensor_tensor(
                    win[:], dec[:], sct[:].unsqueeze(1).to_broadcast([128, H, 128]), ALU.mult),
                    reads=[dec_b, sct_b], writes=[win_b])
                pb.op("pe", lambda e, bf_=bf_, ts=ts: e.transpose(P4[0][:, 384:512], bf_[:, ts], ident[:]),
                      reads=[bfb, ident_b], writes=[P4[1]])
                pb.op("act", lambda e: e.activation(btm[:], P4[0][:, 384:512], AF.Copy), reads=[P4[1]], writes=[btm_b])
                for j in range(6):
                    pb.op("pe", lambda e, j=j, xfm=xfm, ts=ts: e.transpose(P2[0][:, j * 128:(j + 1) * 128], xfm[:, j, ts], ident[:]),
                          reads=[xfb, ident_b], writes=[P2[1]])
                pb.op("act", lambda e: e.activation(xtm[:].rearrange("p h d -> p (h d)"), P2[0][:, 0:768], AF.Copy),
                      reads=[P2[1]], writes=[xtm_b])
                pb.op("dve", lambda e: e.tensor_tensor(xc[:], xtm[:], dt_[:].unsqueeze(2).to_broadcast([128, H, 64]), ALU.mult),
                      reads=[xtm_b, dt_b], writes=[xc_b])
                pb.op("pool", lambda e: e.tensor_tensor(xd[:], xtm[:], dd[:].unsqueeze(2).to_broadcast([128, H, 64]), ALU.mult),
                      reads=[xtm_b, dd_b], writes=[xd_b])
                pb.op("pool", lambda e: e.tensor_tensor(xcd[:], xc[:], dec[:, :, 127].unsqueeze(2).to_broadcast([128, H, 64]), ALU.mult),
                      reads=[xc_b, dec_b], writes=[xcd_b])
                for h in range(H):
                    pb.op("pe", lambda e, h=h: e.matmul(P3[0][:, h * 64:(h + 1) * 64], win[:, h, :], xc[:, h, :], start=True, stop=True),
                          reads=[win_b, xc_b], writes=[P3[1]])
                pb.op("pe", lambda e, cc_=cc_, ts=ts: e.matmul(P1[0][:, 0:512], cc_[:, ts], sbf[:].rearrange("p h d -> p (h d)")[:, 0:512],
                                                               start=True, stop=True), reads=[ccb, sbf_b], writes=[P1[1]])
                pb.op("pe", lambda e, cc_=cc_, ts=ts: e.matmul(P1[0][:, 512:768], cc_[:, ts], sbf[:].rearrange("p h d -> p (h d)")[:, 512:768],
                                                               start=True, stop=True), reads=[ccb, sbf_b], writes=[P1[1]])
                pb.op("dve", lambda e: e.tensor_tensor(
                    y1[:], P1[0][:, 0:768].rearrange("p (h d) -> p h d", h=H), elc[:].unsqueeze(2).to_broadcast([128, H, 64]), ALU.mult),
                    reads=[P1[1], elc_b], writes=[y1_b])
                pb.op("dve", lambda e: e.tensor_tensor(
                    y1[:], P3[0][:, 0:768].rearrange("p (h d) -> p h d", h=H), y1[:], ALU.add), reads=[P3[1], y1_b], writes=[y1_b])
                pb.op("pool", lambda e: e.tensor_tensor(y1[:], y1[:], xd[:], ALU.add), reads=[y1_b, xd_b], writes=[y1_b])
                pb.op("pe", lambda e: e.matmul(P2[0][:, 0:512], btm[:], xcd[:].rearrange("p h d -> p (h d)")[:, 0:512], start=True, stop=True),
                      reads=[btm_b, xcd_b], writes=[P2[1]])
                pb.op("pe", lambda e: e.matmul(P2[0][:, 512:768], btm[:], xcd[:].rearrange("p h d -> p (h d)")[:, 512:768], start=True, stop=True),
                      reads=[btm_b, xcd_b], writes=[P2[1]])
                pb.op("dve", lambda e: e.tensor_tensor(sf[:], sf[:], eend[:].unsqueeze(2).to_broadcast([128, H, 64]), ALU.mult),
                      reads=[sf_b, eend_b], writes=[sf_b])
                pb.op("dve", lambda e: e.tensor_tensor(sf[:], P2[0][:, 0:768].rearrange("p (h d) -> p h d", h=H), sf[:], ALU.add),
                      reads=[P2[1], sf_b], writes=[sf_b])
                pb.op("act", lambda e: e.activation(sbf[:], sf[:], AF.Copy), reads=[sf_b], writes=[sbf_b])
                pb.op("act", lambda e, z=z: e.activation(sz[:], z[:, 0:768], AF.Silu), reads=[zb], writes=[sz_b])
                pb.op("dve", lambda e: e.tensor_tensor(sz[:], sz[:], y1[:].rearrange("p h d -> p (h d)"), ALU.mult),
                      reads=[sz_b, y1_b], writes=[sz_b])
                pb.op("act", lambda e: e.activation(junk[:], sz[:], AF.Square, accum_out=ssq[:]), reads=[sz_b], writes=[junk_b, ssq_b])
                pb.op("dve", lambda e: e.tensor_scalar(ssq[:], ssq[:], 1.0 / 768.0, EPS, ALU.mult, ALU.add), reads=[ssq_b], writes=[ssq_b])
                pb.op("act", lambda e: e.activation(ssq[:], ssq[:], AF.Sqrt), reads=[ssq_b], writes=[ssq_b])
                pb.op("dve", lambda e: e.reciprocal(ssq[:], ssq[:]), reads=[ssq_b], writes=[ssq_b])
                pb.op("dve", lambda e, yot=yot: e.scalar_tensor_tensor(yot[:], sz[:], ssq[:, 0:1], nw[:], ALU.mult, ALU.mult),
                      reads=[sz_b, ssq_b, nw_b], writes=[yob])
                for j in range(6):
                    pb.op("pe", lambda e, j=j, yot=yot: e.transpose(P2[0][:, j * 128:(j + 1) * 128], yot[:, j * 128:(j + 1) * 128], ident[:]),
                          reads=[yob, ident_b], writes=[P2[1]])
                pb.op("act", lambda e, oft=oft: e.activation(oft[:].rearrange("p j t -> p (j t)"), P2[0][:, 0:768], AF.Copy),
                      reads=[P2[1]], writes=[ofb])
                pb.dma("pool", ov[:, :, tt0:tt0 + 128], oft[:], ofb, reads=[ofb])


TWO_PI = float(2.0 * np.pi)


def trig(pb, c, th, th_b, shape, name):
    n = shape[1]
    ki, ki_b = c.sb([128, n], mybir.dt.int32, name + "ki")
    kf, kf_b = c.sb([128, n], F32, name + "kf")
    tr, tr_b = c.sb([128, n], F32, name + "tr")
    mk, mk_b = c.sb([128, n], F32, name + "mk")
    co, co_b = c.sb([128, n], F32, name + "co")
    si, si_b = c.sb([128, n], F32, name + "si")
    pb.op("dve", lambda e: e.tensor_scalar(kf[:], th[:], 1.0 / TWO_PI, None, ALU.mult), reads=[th_b], writes=[kf_b])
    pb.op("dve", lambda e: e.tensor_copy(ki[:], kf[:]), reads=[kf_b], writes=[ki_b])
    pb.op("dve", lambda e: e.tensor_copy(kf[:], ki[:]), reads=[ki_b], writes=[kf_b])
    pb.op("dve", lambda e: e.scalar_tensor_tensor(tr[:], kf[:], -TWO_PI, th[:], ALU.mult, ALU.add), reads=[kf_b, th_b], writes=[tr_b])

    def wrap(t, t_b):
        pb.op("dve", lambda e: e.tensor_single_scalar(mk[:], t[:], float(np.pi), ALU.is_gt), reads=[t_b], writes=[mk_b])
        pb.op("dve", lambda e: e.scalar_tensor_tensor(t[:], mk[:], -TWO_PI, t[:], ALU.mult, ALU.add), reads=[mk_b, t_b], writes=[t_b])
        pb.op("dve", lambda e: e.tensor_single_scalar(mk[:], t[:], float(-np.pi), ALU.is_lt), reads=[t_b], writes=[mk_b])
        pb.op("dve", lambda e: e.scalar_tensor_tensor(t[:], mk[:], TWO_PI, t[:], ALU.mult, ALU.add), reads=[mk_b, t_b], writes=[t_b])

    wrap(tr, tr_b)
    pb.op("act", lambda e: e.activation(si[:], tr[:], AF.Sin), reads=[tr_b], writes=[si_b])
    pb.op("dve", lambda e: e.tensor_scalar(tr[:], tr[:], float(np.pi / 2), None, ALU.add), reads=[tr_b], writes=[tr_b])
    wrap(tr, tr_b)
    pb.op("act", lambda e: e.activation(co[:], tr[:], AF.Sin), reads=[tr_b], writes=[co_b])
    return (co, co_b), (si, si_b)


def stage_s5(pb, es, u, lp_d, lf_d, bt_d, ct_d, dp_d, g32, g16):
    c = Ctx(pb, es)
    T = 256
    NQ = 8
    lp, lp_b = c.sb([128, 3, NQ], F32, "lp")
    lf, lf_b = c.sb([128, 3, 1024], F32, "lf")
    bt, bt_b = c.sb([128, 2, NQ, 128], F32, "bt")
    ct, ct_b = c.sb([128, 2, NQ, 128], F32, "ct")
    dp, dp_b = c.sb([128, 2], F32, "dp")
    for t, b_, d_ in ((lp, lp_b, lp_d), (lf, lf_b, lf_d), (bt, bt_b, bt_d), (ct, ct_b, ct_d), (dp, dp_b, dp_d)):
        pb.dma("sp", t[:], d_, b_, writes=[b_])

    def prep(lay, lay_b, n, nm):
        lr, lr_b = c.sb([128, n], F32, nm + "lr")
        st, st_b = c.sb([128, n], F32, nm + "st")
        th, th_b = c.sb([128, n], F32, nm + "th")
        mg, mg_b = c.sb([128, n], F32, nm + "mg")
        pb.op("dve", lambda e: e.tensor_scalar(lr[:], lay[:, 0, :], -1e-4, None, ALU.min), reads=[lay_b], writes=[lr_b])
        pb.op("act", lambda e: e.activation(st[:], lay[:, 2, :], AF.Exp), reads=[lay_b], writes=[st_b])
        pb.op("dve", lambda e: e.tensor_tensor(th[:], lay[:, 1, :], st[:], ALU.mult), reads=[lay_b, st_b], writes=[th_b])
        pb.op("dve", lambda e: e.tensor_tensor(mg[:], lr[:], st[:], ALU.mult), reads=[lr_b, st_b], writes=[mg_b])
        pb.op("act", lambda e: e.activation(mg[:], mg[:], AF.Exp), reads=[mg_b], writes=[mg_b])
        co, si = trig(pb, c, th, th_b, [128, n], nm)
        return (lr, lr_b), (mg, mg_b), co, si

    (_, _), (rp, rp_b), (c1, c1_b), (s1, s1_b) = prep(lp, lp_b, NQ, "P")
    cosT, cosT_b = c.sb([128, NQ, T], F32, "cosT")
    sinT, sinT_b = c.sb([128, NQ, T], F32, "sinT")
    rt, rt_b = c.sb([128, NQ, T], F32, "rt")
    tm1, tm1_b = c.sb([128, NQ, T // 2], F32, "tm1")
    tm2, tm2_b = c.sb([128, NQ, T // 2], F32, "tm2")
    pb.op("dve", lambda e: e.tensor_copy(cosT[:, :, 0], c1[:]), reads=[c1_b], writes=[cosT_b])
    pb.op("dve", lambda e: e.tensor_copy(sinT[:, :, 0], s1[:]), reads=[s1_b], writes=[sinT_b])
    pb.op("dve", lambda e: e.tensor_copy(rt[:], rp[:].unsqueeze(2).to_broadcast([128, NQ, T])), reads=[rp_b], writes=[rt_b])
    m = 1
    while m < T:
        cm = cosT[:, :, m - 1:m].to_broadcast([128, NQ, m])
        sm = sinT[:, :, m - 1:m].to_broadcast([128, NQ, m])
        pb.op("dve", lambda e, m=m, cm=cm: e.tensor_tensor(tm1[:, :, 0:m], cosT[:, :, 0:m], cm, ALU.mult), reads=[cosT_b], writes=[tm1_b])
        pb.op("dve", lambda e, m=m, sm=sm: e.tensor_tensor(tm2[:, :, 0:m], sinT[:, :, 0:m], sm, ALU.mult), reads=[sinT_b], writes=[tm2_b])
        pb.op("dve", lambda e, m=m: e.tensor_tensor(cosT[:, :, m:2 * m], tm1[:, :, 0:m], tm2[:, :, 0:m], ALU.subtract),
              reads=[tm1_b, tm2_b], writes=[cosT_b])
        pb.op("dve", lambda e, m=m, cm=cm: e.tensor_tensor(tm1[:, :, 0:m], sinT[:, :, 0:m], cm, ALU.mult), reads=[sinT_b, cosT_b], writes=[tm1_b])
        pb.op("dve", lambda e, m=m, sm=sm: e.tensor_tensor(tm2[:, :, 0:m], cosT[:, :, 0:m], sm, ALU.mult), reads=[cosT_b, sinT_b], writes=[tm2_b])
        pb.op("dve", lambda e, m=m: e.tensor_tensor(sinT[:, :, m:2 * m], tm1[:, :, 0:m], tm2[:, :, 0:m], ALU.add),
              reads=[tm1_b, tm2_b], writes=[sinT_b])
        m *= 2

    (lrf, lrf_b), (mgf, mgf_b), (cf, cf_b), (sf_, sf_b) = prep(lf, lf_b, 1024, "F")
    nr, nr_b = c.sb([128, 1024], F32, "nr")
    ni, ni_b = c.sb([128, 1024], F32, "ni")
    dn, dn_b = c.sb([128, 1024], F32, "dn")
    w1, w1_b = c.sb([128, 1024], F32, "w1")
    cre, cre_b = c.sb([128, 1024], F32, "cre")
    cim, cim_b = c.sb([128, 1024], F32, "cim")
    li = lf[:, 1, :]
    pb.op("dve", lambda e: e.tensor_tensor(nr[:], mgf[:], cf[:], ALU.mult), reads=[mgf_b, cf_b], writes=[nr_b])
    pb.op("dve", lambda e: e.tensor_scalar(nr[:], nr[:], -1.0, None, ALU.add), reads=[nr_b], writes=[nr_b])
    pb.op("dve", lambda e: e.tensor_tensor(ni[:], mgf[:], sf_[:], ALU.mult), reads=[mgf_b, sf_b], writes=[ni_b])
    pb.op("dve", lambda e: e.tensor_tensor(dn[:], lrf[:], lrf[:], ALU.mult), reads=[lrf_b], writes=[dn_b])
    pb.op("dve", lambda e: e.tensor_tensor(w1[:], li, li, ALU.mult), reads=[lf_b], writes=[w1_b])
    pb.op("dve", lambda e: e.tensor_tensor(dn[:], dn[:], w1[:], ALU.add), reads=[dn_b, w1_b], writes=[dn_b])
    pb.op("dve", lambda e: e.reciprocal(dn[:], dn[:]), reads=[dn_b], writes=[dn_b])
    pb.op("dve", lambda e: e.tensor_tensor(cre[:], nr[:], lrf[:], ALU.mult), reads=[nr_b, lrf_b], writes=[cre_b])
    pb.op("dve", lambda e: e.tensor_tensor(w1[:], ni[:], li, ALU.mult), reads=[ni_b, lf_b], writes=[w1_b])
    pb.op("dve", lambda e: e.tensor_tensor(cre[:], cre[:], w1[:], ALU.add), reads=[cre_b, w1_b], writes=[cre_b])
    pb.op("dve", lambda e: e.tensor_tensor(cre[:], cre[:], dn[:], ALU.mult), reads=[cre_b, dn_b], writes=[cre_b])
    pb.op("dve", lambda e: e.tensor_tensor(cim[:], ni[:], lrf[:], ALU.mult), reads=[ni_b, lrf_b], writes=[cim_b])
    pb.op("dve", lambda e: e.tensor_tensor(w1[:], nr[:], li, ALU.mult), reads=[nr_b, lf_b], writes=[w1_b])
    pb.op("dve", lambda e: e.tensor_tensor(cim[:], cim[:], w1[:], ALU.subtract), reads=[cim_b, w1_b], writes=[cim_b])
    pb.op("dve", lambda e: e.tensor_tensor(cim[:], cim[:], dn[:], ALU.mult), reads=[cim_b, dn_b], writes=[cim_b])
    bbre, bbre_b = c.sb([128, 1024], BF16, "bbre")
    bbim, bbim_b = c.sb([128, 1024], BF16, "bbim")
    bre = bt[:, 0, :, :].rearrange("p q m -> p (q m)")
    bim = bt[:, 1, :, :].rearrange("p q m -> p (q m)")
    pb.op("dve", lambda e: e.tensor_tensor(nr[:], cre[:], bre, ALU.mult), reads=[cre_b, bt_b], writes=[nr_b])
    pb.op("dve", lambda e: e.tensor_tensor(ni[:], cim[:], bim, ALU.mult), reads=[cim_b, bt_b], writes=[ni_b])
    pb.op("dve", lambda e: e.tensor_tensor(bbre[:], nr[:], ni[:], ALU.subtract), reads=[nr_b, ni_b], writes=[bbre_b])
    pb.op("dve", lambda e: e.tensor_tensor(nr[:], cre[:], bim, ALU.mult), reads=[cre_b, bt_b], writes=[nr_b])
    pb.op("dve", lambda e: e.tensor_tensor(ni[:], cim[:], bre, ALU.mult), reads=[cim_b, bt_b], writes=[ni_b])
    pb.op("dve", lambda e: e.tensor_tensor(bbim[:], nr[:], ni[:], ALU.add), reads=[nr_b, ni_b], writes=[bbim_b])
    cpos, cpos_b = c.sb([128, NQ, 128], BF16, "cpos")
    cneg, cneg_b = c.sb([128, NQ, 128], BF16, "cneg")
    cimn, cimn_b = c.sb([128, NQ, 128], BF16, "cimn")
    pb.op("dve", lambda e: e.tensor_copy(cpos[:], ct[:, 0, :, :]), reads=[ct_b], writes=[cpos_b])
    pb.op("dve", lambda e: e.tensor_scalar(cneg[:], ct[:, 0, :, :], -1.0, None, ALU.mult), reads=[ct_b], writes=[cneg_b])
    pb.op("dve", lambda e: e.tensor_scalar(cimn[:], ct[:, 1, :, :], -1.0, None, ALU.mult), reads=[ct_b], writes=[cimn_b])

    uf = [c.sb([128, 2, T], F32, "uf") for _ in range(2)]
    ub = [c.sb([128, 2, T], BF16, "ub") for _ in range(2)]
    ini, ini_b = c.sb([128, NQ, 2], F32, "ini")
    tA = [c.sb([128, 2, T], F32, "tA") for _ in range(2)]
    tB = [c.sb([128, 2, T], F32, "tB") for _ in range(2)]
    bp = [c.sb([128, 2, T], F32, "bp") for _ in range(2)]
    sp_ = [c.sb([128, 2, T], F32, "sp") for _ in range(2)]
    wc = [c.sb([128, 2, T], BF16, "wc") for _ in range(2)]
    ws = [c.sb([128, 2, T], BF16, "ws") for _ in range(2)]
    tmpc, tmpc_b = c.sb([128, 2], F32, "tmpc")
    yt = [c.sb([128, T], F32, "yt") for _ in range(2)]
    x2 = [c.sb([128, T], F32, "x2") for _ in range(2)]
    go = [c.sb([128, 2, T], F32, "go") for _ in range(2)]
    gb = [c.sb([128, 2, T], BF16, "gb") for _ in range(2)]
    pbu = [c.ps([128, 2, T], F32, "pbu") for _ in range(3)]
    py = [c.ps([128, T], F32, "py") for _ in range(2)]
    uv = u.rearrange("(j p) n -> p j n", p=128)
    g32v = g32.rearrange("(j p) n -> p j n", p=128)
    g16v = g16.rearrange("(j p) n -> p j n", p=128)
    it = 0
    for b in range(2):
        pb.op("dve", lambda e: e.memset(ini[:], 0.0), writes=[ini_b])
        for ch in range(SEQ // T):
            t0 = b * SEQ + ch * T
            u_, u_b = uf[ch % 2]
            ubt, ub_b = ub[ch % 2]
            got, go_b = go[ch % 2]
            gbt, gb_b = gb[ch % 2]
            pb.dma("sp", u_[:], uv[:, :, t0:t0 + T], u_b, writes=[u_b])
            pb.op("act", lambda e, ubt=ubt, u_=u_: e.activation(ubt[:], u_[:], AF.Copy), reads=[u_b], writes=[ub_b])
            for yt_i in range(2):
                pyt, py_b = py[yt_i]
                for qq in range(4):
                    q = yt_i * 4 + qq
                    pu, pu_b = pbu[it % 3]
                    a_, a_b = tA[it % 2]
                    b_, b_b = tB[it % 2]
                    p_, p_b = bp[it % 2]
                    s_, s_b = sp_[it % 2]
                    wc_, wc_b = wc[it % 2]
                    ws_, ws_b = ws[it % 2]
                    it += 1
                    pb.op("pe", lambda e, pu=pu, q=q, ubt=ubt, yt_i=yt_i: e.matmul(
                        pu[:, 0, :], bbre[:, q * 128:(q + 1) * 128], ubt[:, yt_i, :], start=True, stop=True),
                        reads=[bbre_b, ub_b], writes=[pu_b])
                    pb.op("pe", lambda e, pu=pu, q=q, ubt=ubt, yt_i=yt_i: e.matmul(
                        pu[:, 1, :], bbim[:, q * 128:(q + 1) * 128], ubt[:, yt_i, :], start=True, stop=True),
                        reads=[bbim_b, ub_b], writes=[pu_b])
                    cq_ = cosT[:, q, :].unsqueeze(1).to_broadcast([128, 2, T])
                    sq_ = sinT[:, q, :].unsqueeze(1).to_broadcast([128, 2, T])
                    pb.op("dve", lambda e, a_=a_, pu=pu, cq_=cq_: e.tensor_tensor(a_[:], pu[:], cq_, ALU.mult),
                          reads=[pu_b, cosT_b], writes=[a_b])
                    pb.op("dve", lambda e, b_=b_, pu=pu, sq_=sq_: e.tensor_tensor(b_[:], pu[:], sq_, ALU.mult),
                          reads=[pu_b, sinT_b], writes=[b_b])
                    pb.op("dve", lambda e, p_=p_, a_=a_, b_=b_: e.tensor_tensor(p_[:, 0, :], a_[:, 0, :], b_[:, 1, :], ALU.add),
                          reads=[a_b, b_b], writes=[p_b])
                    pb.op("dve", lambda e, p_=p_, a_=a_, b_=b_: e.tensor_tensor(p_[:, 1, :], a_[:, 1, :], b_[:, 0, :], ALU.subtract),
                          reads=[a_b, b_b], writes=[p_b])
                    for ri in range(2):
                        pb.op("dve", lambda e, s_=s_, p_=p_, q=q, ri=ri: e.tensor_tensor_scan(
                            s_[:, ri, :], rt[:, q, :], p_[:, ri, :], ini[:, q, ri:ri + 1], ALU.mult, ALU.add),
                            reads=[rt_b, p_b, ini_b], writes=[s_b])
                    ce = cosT[:, q, T - 1:T]
                    se = sinT[:, q, T - 1:T]
                    pb.op("dve", lambda e, s_=s_, se=se: e.tensor_scalar(tmpc[:, 0:1], s_[:, 1, T - 1:T], se, None, ALU.mult),
                          reads=[s_b, sinT_b], writes=[tmpc_b])
                    pb.op("dve", lambda e, s_=s_, se=se: e.tensor_scalar(tmpc[:, 1:2], s_[:, 0, T - 1:T], se, None, ALU.mult),
                          reads=[s_b, sinT_b], writes=[tmpc_b])
                    pb.op("dve", lambda e, s_=s_, ce=ce, q=q: e.scalar_tensor_tensor(
                        ini[:, q, 0:1], s_[:, 0, T - 1:T], ce, tmpc[:, 0:1], ALU.mult, ALU.subtract),
                        reads=[s_b, cosT_b, tmpc_b], writes=[ini_b])
                    pb.op("dve", lambda e, s_=s_, ce=ce, q=q: e.scalar_tensor_tensor(
                        ini[:, q, 1:2], s_[:, 1, T - 1:T], ce, tmpc[:, 1:2], ALU.mult, ALU.add),
                        reads=[s_b, cosT_b, tmpc_b], writes=[ini_b])
                    pb.op("pool", lambda e, wc_=wc_, s_=s_, cq_=cq_: e.tensor_tensor(wc_[:], s_[:], cq_, ALU.mult),
                          reads=[s_b, cosT_b], writes=[wc_b])
                    pb.op("pool", lambda e, ws_=ws_, s_=s_, sq_=sq_: e.tensor_tensor(ws_[:], s_[:], sq_, ALU.mult),
                          reads=[s_b, sinT_b], writes=[ws_b])
                    mm = ((cpos, cpos_b, wc_, wc_b, 0), (cimn, cimn_b, wc_, wc_b, 1),
                          (cimn, cimn_b, ws_, ws_b, 0), (cneg, cneg_b, ws_, ws_b, 1))
                    for mi, (lt, lt_b, wt, wt_b, ri) in enumerate(mm):
                        pb.op("pe", lambda e, pyt=pyt, lt=lt, wt=wt, ri=ri, q=q, qq=qq, mi=mi: e.matmul(
                            pyt[:], lt[:, q, :], wt[:, ri, :], start=(qq == 0 and mi == 0), stop=(qq == 3 and mi == 3)),
                            reads=[lt_b, wt_b], writes=[py_b])
                y_, y_b = yt[yt_i]
                x_, x_b = x2[yt_i]
                pb.op("dve", lambda e, y_=y_, u_=u_, yt_i=yt_i, pyt=pyt: e.scalar_tensor_tensor(
                    y_[:], u_[:, yt_i, :], dp[:, yt_i:yt_i + 1], pyt[:], ALU.mult, ALU.add),
                    reads=[u_b, dp_b, py_b], writes=[y_b])
                pb.op("pool", lambda e, x_=x_, y_=y_: e.tensor_tensor(x_[:], y_[:], y_[:], ALU.mult), reads=[y_b], writes=[x_b])
                pb.op("pool", lambda e, x_=x_: e.tensor_scalar(x_[:], x_[:], 0.044715, 1.0, ALU.mult, ALU.add), reads=[x_b], writes=[x_b])
                pb.op("pool", lambda e, x_=x_, y_=y_: e.tensor_tensor(x_[:], x_[:], y_[:], ALU.mult), reads=[x_b, y_b], writes=[x_b])
                pb.op("act", lambda e, x_=x_: e.activation(x_[:], x_[:], AF.Sigmoid, scale=1.5957691216057308), reads=[x_b], writes=[x_b])
                pb.op("dve", lambda e, got=got, x_=x_, y_=y_, yt_i=yt_i: e.tensor_tensor(got[:, yt_i, :], x_[:], y_[:], ALU.mult),
                      reads=[x_b, y_b], writes=[go_b])
            pb.op("act", lambda e, gbt=gbt, got=got: e.activation(gbt[:], got[:], AF.Copy), reads=[go_b], writes=[gb_b])
            pb.dma("pool", g32v[:, :, t0:t0 + T], got[:], go_b, reads=[go_b])
            pb.dma("pool", g16v[:, :, t0:t0 + T], gbt[:], gb_b, reads=[gb_b])


def stage_s5fin(pb, es, g32, lin, gate, bg_d, out):
    c = Ctx(pb, es)
    bg, bg_b = c.sb([128, 2], F32, "bg")
    pb.dma("sp", bg[:], bg_d, bg_b, writes=[bg_b])
    gt = [c.sb([128, 2, TC], F32, "gt") for _ in range(2)]
    lt = [c.sb([128, 2, TC], F32, "lt") for _ in range(2)]
    ga = [c.sb([128, 2, TC], F32, "ga") for _ in range(2)]
    ot = [c.sb([128, 2, TC], BF16, "ot") for _ in range(2)]
    r2 = lambda a: a.rearrange("(j p) n -> p j n", p=128)
    for ch in range(NCH):
        sl = slice(ch * TC, (ch + 1) * TC)
        g_, g_b = gt[ch % 2]
        l_, l_b = lt[ch % 2]
        a_, a_b = ga[ch % 2]
        o_, o_b = ot[ch % 2]
        pb.dma("sp", g_[:], r2(g32)[:, :, sl], g_b, writes=[g_b])
        pb.dma("sp", l_[:], r2(lin)[:, :, sl], l_b, writes=[l_b])
        pb.dma("sp", a_[:], r2(gate)[:, :, sl], a_b, writes=[a_b])
        for j in range(2):
            pb.op("act", lambda e, l_=l_, j=j: e.activation(l_[:, j, :], l_[:, j, :], AF.Sigmoid, bias=bg[:, j:j + 1]),
                  reads=[l_b, bg_b], writes=[l_b])
        pb.op("act", lambda e, a_=a_: e.activation(a_[:], a_[:], AF.Silu), reads=[a_b], writes=[a_b])
        pb.op("dve", lambda e, g_=g_, l_=l_: e.tensor_tensor(g_[:], g_[:], l_[:], ALU.mult), reads=[g_b, l_b], writes=[g_b])
        pb.op("dve", lambda e, o_=o_, g_=g_, a_=a_: e.tensor_tensor(o_[:], g_[:], a_[:], ALU.mult), reads=[g_b, a_b], writes=[o_b])
        pb.dma("pool", r2(out)[:, :, sl], o_[:], o_b, reads=[o_b])


S5W, SSDW = 2048, 6144


def l0_weight_maps(p):
    w = p["l0_w_in"]
    maps = []
    o_gate, o_z, o_x = 2048, 4096, 10240
    o_B, o_C, o_dt = 10240 + 6144, 10240 + 6144 + 1024, 18432
    for i in range(NCORES):
        u = w[:, 256 * i:256 * (i + 1)]
        ga = w[:, o_gate + 256 * i:o_gate + 256 * (i + 1)]
        z = w[:, o_z + 768 * i:o_z + 768 * (i + 1)]
        x = w[:, o_x + 768 * i:o_x + 768 * (i + 1)]
        B = w[:, o_B + 128 * i:o_B + 128 * (i + 1)]
        C = w[:, o_C + 128 * i:o_C + 128 * (i + 1)]
        dt = w[:, o_dt + 12 * i:o_dt + 12 * (i + 1)]
        m = {
            "w0": wlay(np.concatenate([u, ga], 1)),
            "w1": wlay(x[:, 0:512]),
            "w2": wlay(np.concatenate([x[:, 512:768], B, C], 1)),
            "w3": wlay(z[:, 0:512]),
            "w4": wlay(np.concatenate([z[:, 512:768], dt], 1)),
        }
        gs = np.arange(16 * i, 16 * (i + 1)).reshape(8, 2)
        lre, lim, lst = p["l0_s5_lambda_re"], p["l0_s5_lambda_im"], p["l0_s5_log_step"]
        lp = np.zeros((128, 3, 8), np.float32)
        lf = np.zeros((3, 8, 128), np.float32)
        bt = np.zeros((128, 2, 8, 128), np.float32)
        ct = np.zeros((128, 2, 8, 128), np.float32)
        dp = np.zeros((128, 2), np.float32)
        for q in range(8):
            for e in range(2):
                g = gs[q, e]
                gl = 2 * q + e
                rows = slice(e * 64, (e + 1) * 64)
                lp[rows, 0, q] = lre[g]
                lp[rows, 1, q] = lim[g]
                lp[rows, 2, q] = lst[g]
                lf[0, q, rows] = lre[g]
                lf[1, q, rows] = lim[g]
                lf[2, q, rows] = lst[g]
                ur = slice((gl % 8) * 16, (gl % 8) * 16 + 16)
                bt[ur, 0, q, rows] = p["l0_s5_b_re"][g].T
                bt[ur, 1, q, rows] = p["l0_s5_b_im"][g].T
                ct[rows, 0, q, ur] = p["l0_s5_c_re"][g].T
                ct[rows, 1, q, ur] = p["l0_s5_c_im"][g].T
                dp[ur, gl // 8] = p["l0_s5_d"][g]
        m["lp"] = lp
        m["lf"] = np.ascontiguousarray(np.broadcast_to(lf.reshape(1, 3, 1024), (128, 3, 1024)))
        m["bt"] = bt
        m["ct"] = ct
        m["dp"] = dp
        ch = np.concatenate([np.arange(768 * i, 768 * (i + 1)), 6144 + np.arange(128 * i, 128 * (i + 1)),
                             7168 + np.arange(128 * i, 128 * (i + 1))])
        m["cw"] = np.ascontiguousarray(p["l0_ssd_conv_w"][:, ch].T.reshape(8, 128, 4).transpose(1, 0, 2))
        m["cb"] = np.ascontiguousarray(p["l0_ssd_conv_b"][ch].reshape(8, 128).T)
        rep = lambda a: np.ascontiguousarray(np.broadcast_to(a.reshape(1, -1), (128, a.size)))
        m["dtb"] = rep(p["l0_ssd_dt_bias"][12 * i:12 * (i + 1)])
        m["alog"] = rep(p["l0_ssd_a_log"][12 * i:12 * (i + 1)])
        m["dd"] = rep(p["l0_ssd_d"][12 * i:12 * (i + 1)])
        m["snw"] = rep(p["l0_ssd_norm_w"][768 * i:768 * (i + 1)])
        maps.append(m)
    return maps


def declare_l0(nc):
    T = {}
    for n, fn in (("w0", 512), ("w1", 512), ("w2", 512), ("w3", 512), ("w4", 268)):
        T[n] = dram(nc, n, (128, 32, fn), F32, "ExternalInput")
    for n, shp in (("lp", (128, 3, 8)), ("lf", (128, 3, 1024)), ("bt", (128, 2, 8, 128)), ("ct", (128, 2, 8, 128)),
                   ("dp", (128, 2)), ("cw", (128, 8, 4)), ("cb", (128, 8)), ("dtb", (128, 12)), ("alog", (128, 12)),
                   ("dd", (128, 12)), ("snw", (128, 768))):
        T[n] = dram(nc, n, shp, F32, "ExternalInput")
    return T


def stages_l0_mix(pb, T, act, ug, xbc, ztm, g32, g16, ssd_out):
    groups = [
        dict(w=T["w0"], fn=512, orient="fm", dst=ug),
        dict(w=T["w1"], fn=512, orient="fm", dst=xbc[0:512, :]),
        dict(w=T["w2"], fn=512, orient="fm", dst=xbc[512:1024, :]),
        dict(w=T["w3"], fn=512, orient="tm", dst=ztm[:, 0:512]),
        dict(w=T["w4"], fn=268, orient="tm", dst=ztm[:, 512:780]),
    ]
    run_stage(pb, stage_linear, act, DM, groups)
    run_stage(pb, stage_s5, ug[0:256, :], T["lp"], T["lf"], T["bt"], T["ct"], T["dp"], g32, g16)
    run_stage(pb, stage_ssd, ztm, xbc, T["cw"], T["cb"], T["dtb"], T["alog"], T["dd"], T["snw"], ssd_out)


def build_l0_mix(nc, pb):
    T = declare_l0(nc)
    act = dram(nc, "act", (DM, NT), BF16, "ExternalInput")
    ug = dram(nc, "ug", (512, NT), F32, "ExternalOutput")
    g32 = dram(nc, "g32", (256, NT), F32, "ExternalOutput")
    g16 = dram(nc, "g16", (256, NT), BF16, "ExternalOutput")
    ssd_out = dram(nc, "ssd_out", (768, NT), BF16, "ExternalOutput")
    xbc = dram(nc, "xbc_i", (1024, NT), F32)
    ztm = dram(nc, "ztm_i", (NT, 780), F32)
    stages_l0_mix(pb, T, act, ug, xbc, ztm, g32, g16, ssd_out)


def build_glu(nc, pb):
    act = dram(nc, "act", (2048, NT), BF16, "ExternalInput")
    wg = dram(nc, "wglu", (128, 16, 256), F32, "ExternalInput")
    bg = dram(nc, "bg", (128, 2), F32, "ExternalInput")
    ug = dram(nc, "ug", (512, NT), F32, "ExternalInput")
    g32 = dram(nc, "g32", (256, NT), F32, "ExternalInput")
    s5o = dram(nc, "s5o", (256, NT), BF16, "ExternalOutput")
    lin = dram(nc, "lin_i", (256, NT), F32)
    run_stage(pb, stage_linear, act, 2048, [dict(w=wg, fn=256, orient="fm", dst=lin)])
    run_stage(pb, stage_s5fin, g32, lin, ug[256:512, :], bg, s5o)


def make_build_outproj(K):
    def build(nc, pb):
        act = dram(nc, "act", (K, NT), BF16, "ExternalInput")
        w = dram(nc, "w", (128, K // 128, 512), F32, "ExternalInput")
        xs = dram(nc, "xs", (512, NT), F32, "ExternalInput")
        xo = dram(nc, "xo", (512, NT), F32, "ExternalOutput")
        ssq = dram(nc, "ssq", (1, NT), F32, "ExternalOutput")
        run_stage(pb, stage_linear, act, K, [dict(w=w, fn=512, orient="fm", dst=xo, resid=xs, ssq=ssq)])
    return build


def build_ssq(nc, pb):
    xs = dram(nc, "xs", (512, NT), F32, "ExternalInput")
    ssq = dram(nc, "ssq", (1, NT), F32, "ExternalOutput")
    run_stage(pb, stage_ssq, xs, ssq)


def make_build_apply(out_dt):
    def build(nc, pb):
        xs = dram(nc, "xs", (512, NT), F32, "ExternalInput")
        ssq_all = dram(nc, "ssq_all", (8, NT), F32, "ExternalInput")
        nw = dram(nc, "nw", (128, 4), F32, "ExternalInput")
        xn = dram(nc, "xn", (512, NT), out_dt, "ExternalOutput")
        run_stage(pb, stage_apply, xs, ssq_all, nw, xn, out_dt)
    return build


def nwlay(w, i):
    return np.ascontiguousarray(w[512 * i:512 * (i + 1)].reshape(4, 128).T)


def cat(res, name):
    return np.concatenate([r[name] for r in res], 0)


def kernel(**p):
    p = {k: np.asarray(v) for k, v in p.items()}
    x = p["x"]
    xT = np.ascontiguousarray(x.reshape(NT, DM).T)
    xs = [np.ascontiguousarray(xT[512 * i:512 * (i + 1)]) for i in range(NCORES)]
    r = launch(build_ssq, [{"xs": xs[i]} for i in range(NCORES)])
    ssq_all = cat(r, "ssq")
    r = launch(make_build_apply(BF16), [{"xs": xs[i], "ssq_all": ssq_all, "nw": nwlay(p["l0_norm_w"], i)} for i in range(NCORES)])
    xn0 = cat(r, "xn")
    maps = l0_weight_maps(p)
    for m in maps:
        m["act"] = xn0
    r = launch(build_l0_mix, maps)
    g_all = cat(r, "g16")
    wglu = p["l0_s5_w_glu"]
    maps2 = [{"act": g_all, "wglu": wlay(wglu[:, 256 * i:256 * (i + 1)]),
              "bg": np.ascontiguousarray(p["l0_s5_b_glu"][256 * i:256 * (i + 1)].reshape(2, 128).T),
              "ug": r[i]["ug"], "g32": r[i]["g32"]} for i in range(NCORES)]
    r2 = launch(build_glu, maps2)
    mixed_all = np.concatenate([np.concatenate([r2[i]["s5o"], r[i]["ssd_out"]], 0) for i in range(NCORES)], 0)
    perm = np.concatenate([np.concatenate([256 * i + np.arange(256), S5W + 768 * i + np.arange(768)]) for i in range(NCORES)])
    wo = p["l0_w_out"][perm]
    r = launch(make_build_outproj(8192), [{"act": mixed_all, "w": wlay(wo[:, 512 * i:512 * (i + 1)]), "xs": xs[i]} for i in range(NCORES)])
    x1s = [r[i]["xo"] for i in range(NCORES)]
    ssq_all = cat(r, "ssq")
    r = launch(make_build_apply(BF16), [{"xs": x1s[i], "ssq_all": ssq_all, "nw": nwlay(p["l1_norm_w"], i)} for i in range(NCORES)])
    xn1 = cat(r, "xn")
    r = launch(build_l1, l1_inmaps(xn1, p["l1_w_in"], p["l1_fox_b_f"]))
    att_all = cat(r, "att")
    wo1 = p["l1_w_out"]
    r = launch(make_build_outproj(4096), [{"act": att_all, "w": wlay(wo1[:, 512 * i:512 * (i + 1)]), "xs": x1s[i]} for i in range(NCORES)])
    x2s = [r[i]["xo"] for i in range(NCORES)]
    ssq_all = cat(r, "ssq")
    r = launch(make_build_apply(F32), [{"xs": x2s[i], "ssq_all": ssq_all, "nw": nwlay(p["final_norm_w"], i)} for i in range(NCORES)])
    outT = cat(r, "xn")
    return np.ascontiguousarray(outT.T).reshape(2, SEQ, DM).astype(np.float32)


def build_fused(nc, pb):
    I = lambda n, shp, dt: dram(nc, n, shp, dt, "ExternalInput")
    D = lambda n, shp, dt: dram(nc, n, shp, dt, "Internal")
    xs = I("xs", (512, NT), F32)
    nw0, nw1, nw2 = (I(n, (128, 4), F32) for n in ("nw0", "nw1", "nw2"))
    T = declare_l0(nc)
    wglu = I("wglu", (128, 16, 256), F32)
    bg = I("bg", (128, 2), F32)
    wo0 = I("wo0", (128, 64, 512), F32)
    W1 = {n: I(n, (128, 32, 512), F32) for n in ("wq", "wk", "wv", "wg")}
    wf = I("wf", (128, 32, 4), F32)
    bf = I("bf", (4, 1), F32)
    wo1 = I("wo1", (128, 32, 512), F32)
    out = dram(nc, "out", (512, NT), F32, "ExternalOutput")

    def ag(src, dst):
        pb.allgather(src, dst)
        pb.flush()

    ssq0, ssq0a = D("ssq0", (1, NT), F32), D("ssq0a", (8, NT), F32)
    xn0, xn0a = D("xn0", (512, NT), BF16), D("xn0a", (DM, NT), BF16)
    run_stage(pb, stage_ssq, xs, ssq0)
    ag(ssq0, ssq0a)
    run_stage(pb, stage_apply, xs, ssq0a, nw0, xn0, BF16)
    ag(xn0, xn0a)
    ug = D("ug", (512, NT), F32)
    xbc = D("xbc", (1024, NT), F32)
    ztm = D("ztm", (NT, 780), F32)
    g32, g16 = D("g32", (256, NT), F32), D("g16", (256, NT), BF16)
    ga = D("ga", (2048, NT), BF16)
    mixed, mixa = D("mixed", (1024, NT), BF16), D("mixa", (8192, NT), BF16)
    lin = D("lin", (256, NT), F32)
    stages_l0_mix(pb, T, xn0a, ug, xbc, ztm, g32, g16, mixed[256:1024, :])
    ag(g16, ga)
    run_stage(pb, stage_linear, ga, 2048, [dict(w=wglu, fn=256, orient="fm", dst=lin)])
    run_stage(pb, stage_s5fin, g32, lin, ug[256:512, :], bg, mixed[0:256, :])
    ag(mixed, mixa)
    x1 = D("x1", (512, NT), F32)
    ssq1, ssq1a = D("ssq1", (1, NT), F32), D("ssq1a", (8, NT), F32)
    xn1, xn1a = D("xn1", (512, NT), BF16), D("xn1a", (DM, NT), BF16)
    run_stage(pb, stage_linear, mixa, 8192, [dict(w=wo0, fn=512, orient="fm", dst=x1, resid=xs, ssq=ssq1)])
    ag(ssq1, ssq1a)
    run_stage(pb, stage_apply, x1, ssq1a, nw1, xn1, BF16)
    ag(xn1, xn1a)
    q, k = D("q_i", (512, NT), BF16), D("k_i", (512, NT), BF16)
    v, g, f = D("v_i", (NT, 512), BF16), D("g_i", (NT, 512), F32), D("f_i", (4, NT), F32)
    att, atta = D("att", (512, NT), BF16), D("atta", (DM, NT), BF16)
    groups = [
        dict(w=W1["wq"], fn=512, orient="fm", dst=q, dt=BF16),
        dict(w=W1["wk"], fn=512, orient="fm", dst=k, dt=BF16),
        dict(w=W1["wv"], fn=512, orient="tm", dst=v, dt=BF16),
        dict(w=W1["wg"], fn=512, orient="tm", dst=g),
        dict(w=wf, fn=4, orient="fm", dst=f),
    ]
    run_stage(pb, stage_linear, xn1a, DM, groups)
    run_stage(pb, stage_fox, q, k, v, g, f, bf, att)
    ag(att, atta)
    x2 = D("x2", (512, NT), F32)
    ssq2, ssq2a = D("ssq2", (1, NT), F32), D("ssq2a", (8, NT), F32)
    run_stage(pb, stage_linear, atta, DM, [dict(w=wo1, fn=512, orient="fm", dst=x2, resid=x1, ssq=ssq2)])
    ag(ssq2, ssq2a)
    run_stage(pb, stage_apply, x2, ssq2a, nw2, out, F32)


def fused_inmaps(p):
    x = p["x"]
    xT = np.ascontiguousarray(x.reshape(NT, DM).T)
    maps = l0_weight_maps(p)
    l1m = l1_inmaps(None, p["l1_w_in"], p["l1_fox_b_f"])
    perm = np.concatenate([np.concatenate([256 * i + np.arange(256), S5W + 768 * i + np.arange(768)]) for i in range(NCORES)])
    wo = p["l0_w_out"][perm]
    for i in range(NCORES):
        m = maps[i]
        m["xs"] = np.ascontiguousarray(xT[512 * i:512 * (i + 1)])
        m["nw0"] = nwlay(p["l0_norm_w"], i)
        m["nw1"] = nwlay(p["l1_norm_w"], i)
        m["nw2"] = nwlay(p["final_norm_w"], i)
        m["wglu"] = wlay(p["l0_s5_w_glu"][:, 256 * i:256 * (i + 1)])
        m["bg"] = np.ascontiguousarray(p["l0_s5_b_glu"][256 * i:256 * (i + 1)].reshape(2, 128).T)
        m["wo0"] = wlay(wo[:, 512 * i:512 * (i + 1)])
        for n in ("wq", "wk", "wv", "wg", "wf", "bf"):
            m[n] = l1m[i][n]
        m["wo1"] = wlay(p["l1_w_out"][:, 512 * i:512 * (i + 1)])
    return maps


def kernel_fused(**p):
    p = {k: np.asarray(v) for k, v in p.items()}
    r = launch(build_fused, fused_inmaps(p))
    outT = cat(r, "out")
    return np.ascontiguousarray(outT.T).reshape(2, SEQ, DM).astype(np.float32)
```
